# Optimizing a Trainium2 kernel written in Bass

```python
import jax, jax.numpy as jnp
from jax import lax
import numpy as np

D_MODEL = 1024
BATCH = 2
SEQ = 16384
DEPTH = 2
DEC_BATCH = 4
DEC_SEQ = 4096
PAST_LEN = 128

GRID_W = 64
N_MIXERS = 2
N_MLA_LAYERS = (DEPTH + 1) // 2
N_NA_LAYERS = DEPTH // 2
MLA_HEADS = 16
Q_LORA = 384
KV_LORA = 256
QK_NOPE = 128
QK_ROPE = 64
V_HEAD = 128
ROPE_THETA = 10000.0
Q_BLOCK = 128
NA_HEADS = 16
NA_HEAD_DIM = D_MODEL // NA_HEADS
NA_KH = 8
NA_KW = 16
D_FF = 4 * D_MODEL
EPS = 1e-6

kernel_name = "hybrid_mla_natten_encoder"


def rms_norm(x, g):
    xf = x.astype(jnp.float32)
    y = xf * lax.rsqrt(jnp.mean(xf * xf, axis=-1, keepdims=True) + EPS)
    return (y * g.astype(jnp.float32)).astype(x.dtype)


def rope_tables(s, dtype):
    inv = ROPE_THETA ** (-jnp.arange(0, QK_ROPE, 2, dtype=jnp.float32) / QK_ROPE)
    ang = jnp.arange(s, dtype=jnp.float32)[:, None] * inv[None, :]
    return jnp.cos(ang).astype(dtype), jnp.sin(ang).astype(dtype)


def apply_rope(x, cos, sin):
    x1, x2 = jnp.split(x, 2, axis=-1)
    return jnp.concatenate([x1 * cos - x2 * sin, x1 * sin + x2 * cos], axis=-1)


def mla(x, w_dq, q_norm, w_uq, w_dkv, kv_norm, w_ukv, w_o):
    b, s, _ = x.shape
    cos, sin = rope_tables(s, x.dtype)
    c_q = rms_norm(x @ w_dq, q_norm)
    q = (c_q @ w_uq).reshape(b, s, MLA_HEADS, QK_NOPE + QK_ROPE)
    q_nope = q[..., :QK_NOPE]
    q_rope = apply_rope(q[..., QK_NOPE:], cos[:, None], sin[:, None])
    kv_a = x @ w_dkv
    c_kv = rms_norm(kv_a[..., :KV_LORA], kv_norm)
    k_rope = apply_rope(kv_a[..., KV_LORA:], cos, sin)
    kv = (c_kv @ w_ukv).reshape(b, s, MLA_HEADS, QK_NOPE + V_HEAD)
    k_nope, v = kv[..., :QK_NOPE], kv[..., QK_NOPE:]
    scale = (QK_NOPE + QK_ROPE) ** -0.5
    nb = s // Q_BLOCK

    def q_block(args):
        qn, qr = args
        sc = jnp.einsum('bqhd,bkhd->bhqk', qn, k_nope) + jnp.einsum('bqhd,bkd->bhqk', qr, k_rope)
        p = jax.nn.softmax(sc.astype(jnp.float32) * scale, axis=-1).astype(v.dtype)
        return jnp.einsum('bhqk,bkhd->bqhd', p, v)

    def to_blocks(t):
        return jnp.moveaxis(t.reshape(b, nb, Q_BLOCK, *t.shape[2:]), 1, 0)

    o = lax.map(q_block, (to_blocks(q_nope), to_blocks(q_rope)))
    o = jnp.moveaxis(o, 0, 1).reshape(b, s, MLA_HEADS * V_HEAD)
    return o @ w_o


def neighbourhood_attention(x, w_qkv, rpb, w_o):
    b, s, _ = x.shape
    rows = s // GRID_W
    kh = min(NA_KH, rows)
    qkv = (x @ w_qkv).reshape(b, rows, GRID_W, 3, NA_HEADS, NA_HEAD_DIM)
    q, k, v = qkv[:, :, :, 0], qkv[:, :, :, 1], qkv[:, :, :, 2]
    cols = np.arange(GRID_W)
    col_start = np.clip(cols - NA_KW // 2, 0, GRID_W - NA_KW)
    col_idx_np = col_start[:, None] + np.arange(NA_KW)[None, :]
    col_idx = jnp.asarray(col_idx_np, dtype=jnp.int32)
    rel_col = jnp.asarray(col_idx_np - cols[:, None] + NA_KW - 1, dtype=jnp.int32)
    scale = NA_HEAD_DIM ** -0.5

    def row_block(args):
        r, q_r = args
        r0 = jnp.clip(r - kh // 2, 0, rows - kh)
        k_rows = lax.dynamic_slice_in_dim(k, r0, kh, axis=1)
        v_rows = lax.dynamic_slice_in_dim(v, r0, kh, axis=1)
        k_win = k_rows[:, :, col_idx]
        v_win = v_rows[:, :, col_idx]
        rel_row = r0 + jnp.arange(kh, dtype=jnp.int32) - r + NA_KH - 1
        bias = rpb[:, rel_row[None, :, None], rel_col[:, None, :]]
        sc = (jnp.einsum('bqhd,biqjhd->bhqij', q_r, k_win).astype(jnp.float32) * scale
              + bias.astype(jnp.float32)[None])
        p = jax.nn.softmax(sc.reshape(b, NA_HEADS, GRID_W, kh * NA_KW), axis=-1)
        p = p.reshape(sc.shape).astype(v.dtype)
        return jnp.einsum('bhqij,biqjhd->bqhd', p, v_win)

    o = lax.map(row_block, (jnp.arange(rows, dtype=jnp.int32), jnp.moveaxis(q, 1, 0)))
    o = jnp.moveaxis(o, 0, 1).reshape(b, s, NA_HEADS * NA_HEAD_DIM)
    return o @ w_o


def sq_relu_mlp(x, w1, w2):
    h = jax.nn.relu(x @ w1)
    return (h * h) @ w2


def trunk(x, attn_norm, mlp_norm, final_norm, mla_w_dq, mla_q_norm, mla_w_uq, mla_w_dkv,
          mla_kv_norm, mla_w_ukv, mla_w_o, na_w_qkv, na_rpb, na_w_o, mlp_w1, mlp_w2):
    for i in range(DEPTH):
        h = rms_norm(x, attn_norm[i])
        j = i // N_MIXERS
        if i % N_MIXERS == 0:
            h = mla(h, mla_w_dq[j], mla_q_norm[j], mla_w_uq[j], mla_w_dkv[j],
                    mla_kv_norm[j], mla_w_ukv[j], mla_w_o[j])
        else:
            h = neighbourhood_attention(h, na_w_qkv[j], na_rpb[j], na_w_o[j])
        x = x + h
        x = x + sq_relu_mlp(rms_norm(x, mlp_norm[i]), mlp_w1[i], mlp_w2[i])
    return rms_norm(x, final_norm)


def setup_inputs(seed: int = 0) -> dict:
    key = jax.random.key(seed)
    ks = jax.random.split(key, 20)
    f32 = jnp.float32

    def nrm(k, shape, scale):
        return jax.random.normal(k, shape, f32) * scale

    def gain(k, shape):
        return 1.0 + 0.01 * jax.random.normal(k, shape, f32)

    return {
        "x_prompt": jax.random.normal(ks[0], (BATCH, SEQ, D_MODEL), f32),
        "x_sample": jax.random.normal(ks[1], (DEC_BATCH, DEC_SEQ, D_MODEL), f32),
        "attn_norm": gain(ks[2], (DEPTH, D_MODEL)),
        "mlp_norm": gain(ks[3], (DEPTH, D_MODEL)),
        "final_norm": gain(ks[4], (D_MODEL,)),
        "mla_w_dq": nrm(ks[5], (N_MLA_LAYERS, D_MODEL, Q_LORA), D_MODEL ** -0.5),
        "mla_q_norm": gain(ks[6], (N_MLA_LAYERS, Q_LORA)),
        "mla_w_uq": nrm(ks[7], (N_MLA_LAYERS, Q_LORA, MLA_HEADS * (QK_NOPE + QK_ROPE)), Q_LORA ** -0.5),
        "mla_w_dkv": nrm(ks[8], (N_MLA_LAYERS, D_MODEL, KV_LORA + QK_ROPE), D_MODEL ** -0.5),
        "mla_kv_norm": gain(ks[9], (N_MLA_LAYERS, KV_LORA)),
        "mla_w_ukv": nrm(ks[10], (N_MLA_LAYERS, KV_LORA, MLA_HEADS * (QK_NOPE + V_HEAD)), KV_LORA ** -0.5),
        "mla_w_o": nrm(ks[11], (N_MLA_LAYERS, MLA_HEADS * V_HEAD, D_MODEL), (MLA_HEADS * V_HEAD) ** -0.5),
        "na_w_qkv": nrm(ks[12], (N_NA_LAYERS, D_MODEL, 3 * NA_HEADS * NA_HEAD_DIM), D_MODEL ** -0.5),
        "na_rpb": nrm(ks[13], (N_NA_LAYERS, NA_HEADS, 2 * NA_KH - 1, 2 * NA_KW - 1), 0.02),
        "na_w_o": nrm(ks[14], (N_NA_LAYERS, NA_HEADS * NA_HEAD_DIM, D_MODEL), (NA_HEADS * NA_HEAD_DIM) ** -0.5),
        "mlp_w1": nrm(ks[15], (DEPTH, D_MODEL, D_FF), D_MODEL ** -0.5),
        "mlp_w2": nrm(ks[16], (DEPTH, D_FF, D_MODEL), D_FF ** -0.5),
    }


def reference(x_prompt, x_sample, attn_norm, mlp_norm, final_norm, mla_w_dq, mla_q_norm, mla_w_uq,
              mla_w_dkv, mla_kv_norm, mla_w_ukv, mla_w_o, na_w_qkv, na_rpb, na_w_o, mlp_w1, mlp_w2):
    y_prompt = trunk(x_prompt, attn_norm, mlp_norm, final_norm, mla_w_dq, mla_q_norm, mla_w_uq,
                     mla_w_dkv, mla_kv_norm, mla_w_ukv, mla_w_o, na_w_qkv, na_rpb, na_w_o, mlp_w1, mlp_w2)
    y_sample = trunk(x_sample, attn_norm, mlp_norm, final_norm, mla_w_dq, mla_q_norm, mla_w_uq,
                     mla_w_dkv, mla_kv_norm, mla_w_ukv, mla_w_o, na_w_qkv, na_rpb, na_w_o, mlp_w1, mlp_w2)
    return (y_prompt, y_sample)
```

```python
import numpy as np
from contextlib import ExitStack
import concourse.bass as bass
import concourse.mybir as mybir
from concourse.bass_utils import run_bass_kernel_spmd

F32 = mybir.dt.float32
BF16 = mybir.dt.bfloat16
AF = mybir.ActivationFunctionType
ALU = mybir.AluOpType

D = 1024
DFF = 4096
NH = 16
EPS = 1e-6
GRID_W = 64
GROUPS = [dict(S=16384, R=72, OWN=64), dict(S=4096, R=40, OWN=32)]


def _fix_groups():
    for _g in GROUPS:
        _g["W"] = _g["R"] * GRID_W
        _g["NT"] = _g["S"] // 128


_fix_groups()
NPHASES = [9]
import os
KVSTOP = int(os.environ.get("KVSTOP", "99"))
G_ATTN0, G_MLP0, G_ATTN1, G_MLP1, G_FIN, G_QN, G_KVN = 0, 1024, 2048, 3072, 4096, 5120, 5504
GW = 5760
NEG = -30000.0


class Sem:
    ALL = []

    def __init__(self, h):
        self.h = h
        self.cnt = 0
        self.uid = len(Sem.ALL)
        Sem.ALL.append(self)


class Tok:
    __slots__ = ("sem", "val")

    def __init__(self, sem, val):
        self.sem = sem
        self.val = val


class Buf:
    def __init__(self, name, sem=None, excl=None):
        self.name = name
        self.w = None
        self.r = {}
        self.sem = sem
        self.excl = name.startswith("p") or name.startswith("np") if excl is None else excl


class Eng:
    def __init__(self, name, e, sem, inorder=False):
        self.name = name
        self.e = e
        self.sem = sem
        self.seen = {}
        self.pending = []
        self.inorder = inorder

    def wait(self, t):
        if t is None:
            return
        if t.sem is self.sem and self.inorder:
            return
        if t.val is None:
            raise RuntimeError("wait on unresolved token (%s)" % self.name)
        k = t.sem.uid
        if self.seen.get(k, 0) >= t.val:
            return
        self.e.wait_ge(t.sem.h, t.val)
        self.seen[k] = t.val


class K:
    def __init__(self, nc, es):
        self.nc = nc
        self.es = es
        self.sempool = []
        self.nsem = 0
        mk = self.newsem
        self.PE = Eng("pe", nc.tensor, mk(), inorder=True)
        self.ACT = Eng("act", nc.scalar, mk())
        self.DVE = Eng("dve", nc.vector, mk())
        self.POOL = Eng("pool", nc.gpsimd, mk())
        self.SP = Eng("sp", nc.sync, mk())
        self.engs = [self.PE, self.ACT, self.DVE, self.POOL, self.SP]
        self.bar = mk()
        self.dma_out = []

    def newsem(self):
        self.nsem += 1
        h = self.es.enter_context(self.nc.semaphore("s%d" % self.nsem))
        return Sem(h)

    def getsem(self):
        if self.sempool:
            return self.sempool.pop()
        return self.newsem()

    def putsem(self, s):
        self.sempool.append(s)

    def _deps(self, E, reads, writes, skipsem=None):
        for b in reads:
            if b.w is not None and b.w.sem is not skipsem:
                E.wait(b.w)
            if b.excl:
                for t in b.r.values():
                    if t.sem is not E.sem:
                        E.wait(t)
        for b in writes:
            if b.w is not None and b.w.sem is not skipsem:
                E.wait(b.w)
            for t in b.r.values():
                E.wait(t)

    def _post(self, tok, reads, writes):
        for b in reads:
            b.r[tok.sem.uid] = tok
        for b in writes:
            b.w = tok
            b.r = {}

    def op(self, E, fn, reads=(), writes=(), inc=True):
        self._deps(E, reads, writes)
        ins = fn(E.e)
        tok = Tok(E.sem, None)
        if inc:
            E.sem.cnt += 1
            ins.then_inc(E.sem.h, 1)
            tok.val = E.sem.cnt
            for p in E.pending:
                p.val = E.sem.cnt
            E.pending = []
        else:
            E.pending.append(tok)
        self._post(tok, reads, writes)
        return tok

    def dma(self, Q, out, in_, sembuf, reads=(), writes=(), chain=False, store=False):
        sem = sembuf.sem
        self._deps(Q, reads, writes, skipsem=sem if chain else None)
        ins = Q.e.dma_start(out=out, in_=in_)
        sem.cnt += 16
        ins.then_inc(sem.h, 16)
        tok = Tok(sem, sem.cnt)
        self._post(tok, reads, writes)
        if store:
            self.dma_out.append(tok)
        return tok

    def barrier(self):
        SP = self.SP
        for E in self.engs:
            if E is SP:
                continue
            if E.pending:
                raise RuntimeError("pending tokens at barrier on " + E.name)
            if E.sem.cnt > 0:
                SP.wait(Tok(E.sem, E.sem.cnt))
        last = {}
        for t in self.dma_out:
            if t.sem.uid not in last or last[t.sem.uid].val < t.val:
                last[t.sem.uid] = t
        for t in last.values():
            SP.wait(t)
        self.dma_out = []
        self.bar.cnt += 1
        SP.e.sem_inc(self.bar.h, 1)
        for E in self.engs:
            if E is SP:
                continue
            E.e.wait_ge(self.bar.h, self.bar.cnt)


class Scope:
    CNT = 0

    def __init__(self, k):
        self.k = k
        self.es = ExitStack()
        self.sems = []
        self.n = 0
        Scope.CNT += 1
        self.uid = Scope.CNT

    def sb(self, name, shape, dt, dma=False, fresh=False):
        self.n += 1
        t = self.es.enter_context(self.k.nc.sbuf_tensor("%s_%d" % (name, self.uid), shape, dt))
        b = Buf(name)
        if dma and fresh:
            b.sem = self.k.newsem()
        elif dma:
            b.sem = self.k.getsem()
            self.sems.append(b.sem)
        return t, b

    def ps(self, name, shape, dt):
        t = self.es.enter_context(self.k.nc.psum_tensor("%s_%d" % (name, self.uid), shape, dt))
        return t

    def close(self):
        for s in self.sems:
            self.k.putsem(s)
        self.es.close()


def build_program():
    nc = bass.Bass("TRN2", target_bir_lowering=False)
    es = ExitStack()

    def din(name, shape, dt=F32):
        return nc.dram_tensor(name, list(shape), dt, kind="ExternalInput").ap()

    def dscr(name, shape, dt):
        return nc.dram_tensor(name, list(shape), dt, kind="Internal").ap()

    def dout(name, shape, dt=F32):
        return nc.dram_tensor(name, list(shape), dt, kind="ExternalOutput").ap()

    I = {}
    for g, cfg in enumerate(GROUPS):
        S, W, NT = cfg["S"], cfg["W"], cfg["NT"]
        I["xs%d" % g] = din("xs%d" % g, [S, D])
        I["xw%d" % g] = din("xw%d" % g, [W, D])
        I["ropek%d" % g] = din("ropek%d" % g, [128, NT, 128])
        I["ropeq%d" % g] = din("ropeq%d" % g, [128, 2, W])
    I["gains"] = din("gains", [128, GW])
    I["ident"] = din("ident", [128, 128])
    I["w_dkv"] = din("w_dkv", [D, 320])
    I["w_dq"] = din("w_dq", [D, 384])
    I["w_uq"] = din("w_uq", [384, NH * 384])
    I["w_uk"] = din("w_uk", [256, NH * 128])
    I["w_uv"] = din("w_uv", [256, NH * 128])
    I["w_o"] = din("w_o", [2048, D])
    I["w1"] = din("w1", [2, D, DFF])
    I["w2"] = din("w2", [2, DFF, D])
    I["na_qkv"] = din("na_qkv", [D, 3 * D])
    I["na_o"] = din("na_o", [D, D])
    I["nab"] = din("nab", [128, 8, 14, 2, 64])
    I["namask"] = din("namask", [128, 64])

    SCR = []
    OUT = []
    for g, cfg in enumerate(GROUPS):
        S, W, NT = cfg["S"], cfg["W"], cfg["NT"]
        SCR.append(dict(
            Kd=dscr("Kd%d" % g, [NH, 128, S], BF16),
            Vd=dscr("Vd%d" % g, [NH, 128, NT * 128], BF16),
            Qn=dscr("Qn%d" % g, [NH, 128, W], BF16),
            Qr=dscr("Qr%d" % g, [NH, 128, W], BF16),
            Od=dscr("Od%d" % g, [NH, 128, W], BF16),
            X1=dscr("X1%d" % g, [W, D], F32),
            X2=dscr("X2%d" % g, [W, D], F32),
            X3=dscr("X3%d" % g, [W, D], F32),
            NQ=dscr("NQ%d" % g, [8, 128, W], BF16),
            NK=dscr("NK%d" % g, [8, 128, W], BF16),
            NV=dscr("NV%d" % g, [8, 128, W], BF16),
            NO=dscr("NO%d" % g, [NH, 64, W], BF16),
        ))
        OUT.append(dout("y%d" % g, [W, D]))

    k = K(nc, es)
    PE, ACT, DVE, POOL, SP = k.PE, k.ACT, k.DVE, k.POOL, k.SP
    es.enter_context(nc.Block())

    glob = Scope(k)
    ident, identb = glob.sb("ident", [128, 128], BF16, dma=True, fresh=True)
    onesf, onesfb = glob.sb("onesf", [128, 128], F32)
    onesb, onesbb = glob.sb("onesb", [128, 64], BF16)
    krsc = Scope(k)
    krT = []
    for g, cfg in enumerate(GROUPS):
        krT.append(krsc.sb("krT%d" % g, [128, cfg["S"] // 2], BF16))
    t_id = k.dma(POOL, ident[:], I["ident"][:, :], identb, writes=[identb])
    t_of = k.op(POOL, lambda e: e.memset(onesf[:], 1.0), writes=[onesfb])
    t_ob = k.op(POOL, lambda e: e.memset(onesb[:], 1.0), writes=[onesbb])
    for E in (PE,):
        E.wait(t_id)
        E.wait(t_of)
        E.wait(t_ob)

    def load_weight(sc, name, src3, shape, engines):
        t, b = sc.sb(name, shape, BF16, dma=True, fresh=True)
        nchunk = shape[1]
        tok = None
        for c in range(nchunk):
            tok = k.dma(POOL, t[:, c], src3[:, c], b, writes=[b], chain=True)
        for E in engines:
            E.wait(tok)
        return t

    def load_gain(sc, name, off, n, engines):
        t, b = sc.sb(name, [128, n], F32, dma=True)
        tok = k.dma(SP, t[:], I["gains"][:, off:off + n], b, writes=[b])
        for E in engines:
            E.wait(tok)
        return t

    def norm_block(sc_bufs, x, xb, T, gain, hT, hTb, pH, pHb):
        junk, junkb, st, stb, hb_t, hb_b = sc_bufs
        for j in range(T):
            k.op(ACT, lambda e, j=j: e.activation(out=junk[:], in_=x[:, j, :], func=AF.Square,
                                                  accum_out=st[:, j:j + 1]),
                 reads=[xb], writes=[junkb, stb])
        k.op(DVE, lambda e: e.tensor_scalar(out=st[:, 8:8 + T], in0=st[:, 0:T], scalar1=1.0 / D, scalar2=EPS,
                                            op0=ALU.mult, op1=ALU.add), reads=[], writes=[stb])
        k.op(ACT, lambda e: e.activation(out=st[:, 16:16 + T], in_=st[:, 8:8 + T], func=AF.Sqrt),
             reads=[], writes=[stb])
        k.op(DVE, lambda e: e.reciprocal(out=st[:, 24:24 + T], in_=st[:, 16:16 + T]), reads=[], writes=[stb])
        for j in range(T):
            hj, hjb = hb_t[j % 2], hb_b[j % 2]
            k.op(DVE, lambda e, j=j, hj=hj: e.scalar_tensor_tensor(
                out=hj[:], in0=x[:, j, :], scalar=st[:, 24 + j:25 + j], in1=gain[:],
                op0=ALU.mult, op1=ALU.mult), reads=[xb, stb], writes=[hjb])
            p, pb = pH[j % 2], pHb[j % 2]
            for c in range(8):
                k.op(PE, lambda e, c=c, hj=hj, p=p: e.transpose(out=p[:, c * 128:(c + 1) * 128],
                                                                 in_=hj[:, c * 128:(c + 1) * 128], identity=ident[:]),
                     reads=[hjb], writes=[pb], inc=(c == 7))
            k.op(ACT, lambda e, j=j, p=p: e.activation(
                out=hT[:, :, j * 128:(j + 1) * 128], in_=p[:].rearrange("p (c t) -> p c t", c=8), func=AF.Copy),
                reads=[pb], writes=[hTb])

    def rstd_small(st, stb, T, n, base):
        k.op(DVE, lambda e: e.tensor_scalar(out=st[:, base + 8:base + 8 + T], in0=st[:, base:base + T],
                                            scalar1=1.0 / n, scalar2=EPS, op0=ALU.mult, op1=ALU.add),
             reads=[], writes=[stb])
        k.op(ACT, lambda e: e.activation(out=st[:, base + 16:base + 16 + T], in_=st[:, base + 8:base + 8 + T],
                                         func=AF.Sqrt), reads=[], writes=[stb])
        k.op(DVE, lambda e: e.reciprocal(out=st[:, base + 24:base + 24 + T], in_=st[:, base + 16:base + 16 + T]),
             reads=[], writes=[stb])

    def norm_scope(sc):
        junk, junkb = sc.sb("junk", [128, D], BF16)
        st, stb = sc.sb("st", [128, 96], F32)
        h0, h0b = sc.sb("h0", [128, D], BF16)
        h1, h1b = sc.sb("h1", [128, D], BF16)
        return (junk, junkb, st, stb, [h0, h1], [h0b, h1b])

    def phase_kv():
        sc = Scope(k)
        wdkv = load_weight(sc, "wdkv", I["w_dkv"].rearrange("(c p) n -> p c n", p=128), [128, 8, 320], [PE])
        wuk = load_weight(sc, "wuk", I["w_uk"].rearrange("(c p) n -> p c n", p=128), [128, 2, 2048], [PE])
        wuv = load_weight(sc, "wuv", I["w_uv"].rearrange("(c p) n -> p c n", p=128), [128, 2, 2048], [PE])
        g_attn = load_gain(sc, "g_attn", G_ATTN0, D, [DVE])
        g_kv = load_gain(sc, "g_kv", G_KVN, 256, [DVE])
        nb = norm_scope(sc)
        st, stb = nb[2], nb[3]
        xs_t = [sc.sb("xs%d" % i, [128, 4, D], F32, dma=True) for i in range(2)]
        rp_t = [sc.sb("rp%d" % i, [128, 4, 128], F32, dma=True) for i in range(3)]
        hT_t = [sc.sb("hT%d" % i, [128, 8, 512], BF16) for i in range(2)]
        kva, kvab = sc.sb("kva", [128, 4, 320], F32)
        ckv, ckvb = sc.sb("ckv", [128, 4, 256], BF16)
        t1, t1b = sc.sb("t1", [128, 4, 64], F32)
        t2, t2b = sc.sb("t2", [128, 4, 64], F32)
        krb_t, krb_b = sc.sb("krb", [128, 4, 128], BF16)
        k.op(POOL, lambda e: e.memset(krb_t[:], 0.0), writes=[krb_b])
        ckvT, ckvTb = sc.sb("ckvT", [128, 2, 512], BF16)
        Kst = [sc.sb("Kst%d" % i, [128, NH, 512], BF16, dma=True) for i in range(2)]
        Vst = [sc.sb("Vst%d" % i, [128, NH, 4, 128], BF16, dma=True) for i in range(2)]
        pH = [sc.ps("pH%d" % i, [128, 1024], BF16) for i in range(2)]
        pHb = [Buf("pH%d" % i) for i in range(2)]
        pKV = [sc.ps("pKV%d" % i, [128, 512], F32) for i in range(2)]
        pKVb = [Buf("pKV%d" % i) for i in range(2)]
        pCT = sc.ps("pCT", [128, 1024], BF16)
        pCTb = Buf("pCT")
        pKR = sc.ps("pKR", [128, 1024], BF16)
        pKRb = Buf("pKR")
        pO = [sc.ps("pO%d" % i, [128, 512], F32) for i in range(2)]
        pOb = [Buf("pO%d" % i) for i in range(2)]

        blocks = []
        for g, cfg in enumerate(GROUPS):
            for blk in range(cfg["S"] // 512):
                blocks.append((g, blk))

        def issue_load(i):
            g, blk = blocks[i]
            NT = GROUPS[g]["NT"]
            xt, xb = xs_t[i % 2]
            src = I["xs%d" % g].rearrange("(p n) d -> p n d", n=NT)
            for j in range(4):
                k.dma(SP, xt[:, j, :], src[:, blk * 4 + j, :], xb, writes=[xb], chain=True)
            rt, rb = rp_t[i % 3]
            k.dma(SP, rt[:], I["ropek%d" % g][:, blk * 4:blk * 4 + 4, :], rb, writes=[rb])

        evac = [0]

        def evac_copy(out, in_, reads, writes):
            evac[0] += 1
            if evac[0] % 2 == 0:
                return k.op(ACT, lambda e: e.activation(out=out, in_=in_, func=AF.Copy), reads=reads, writes=writes)
            return k.op(DVE, lambda e: e.tensor_copy(out=out, in_=in_), reads=reads, writes=writes)

        def front(i):
            xt, xb = xs_t[i % 2]
            hT, hTb = hT_t[i % 2]
            norm_block(nb, xt, xb, 4, g_attn, hT, hTb, pH, pHb)

        def back(i):
            g, blk = blocks[i]
            S, NT = GROUPS[g]["S"], GROUPS[g]["NT"]
            rt, rb = rp_t[i % 3]
            hT, hTb = hT_t[i % 2]
            for j in range(4):
                p, pb = pKV[j % 2], pKVb[j % 2]
                for c in range(8):
                    k.op(PE, lambda e, c=c, j=j, p=p: e.matmul(p[:, 0:320], lhsT=hT[:, c, j * 128:(j + 1) * 128],
                                                                rhs=wdkv[:, c, :], start=(c == 0), stop=(c == 7)),
                         reads=[hTb], writes=[pb], inc=(c == 7))
                k.op(ACT, lambda e, j=j, p=p: e.activation(out=nb[0][:, 0:256], in_=p[:, 0:256], func=AF.Square,
                                                           accum_out=st[:, 32 + j:33 + j]),
                     reads=[pb], writes=[nb[1], stb])
                k.op(DVE, lambda e, j=j, p=p: e.tensor_copy(out=kva[:, j, :], in_=p[:, 0:320]),
                     reads=[pb], writes=[kvab])
            rstd_small(st, stb, 4, 256, 32)
            for j in range(4):
                k.op(DVE, lambda e, j=j: e.scalar_tensor_tensor(
                    out=ckv[:, j, :], in0=kva[:, j, 0:256], scalar=st[:, 56 + j:57 + j], in1=g_kv[:],
                    op0=ALU.mult, op1=ALU.mult), reads=[kvab, stb], writes=[ckvb])
            k.op(DVE, lambda e: e.tensor_tensor(out=t1[:], in0=kva[:, :, 256:320], in1=rt[:, :, 0:64], op=ALU.mult),
                 reads=[kvab, rb], writes=[t1b])
            k.op(DVE, lambda e: e.tensor_tensor(out=t2[:, :, 0:32], in0=kva[:, :, 288:320], in1=rt[:, :, 64:96],
                                                op=ALU.mult), reads=[kvab, rb], writes=[t2b])
            k.op(DVE, lambda e: e.tensor_tensor(out=t2[:, :, 32:64], in0=kva[:, :, 256:288], in1=rt[:, :, 96:128],
                                                op=ALU.mult), reads=[kvab, rb], writes=[t2b])
            for par in range(2):
                k.op(DVE, lambda e, par=par: e.tensor_tensor(
                    out=krb_t[:].rearrange("p (a b) c -> p a b c", b=2)[:, :, par, par * 64:(par + 1) * 64],
                    in0=t1[:].rearrange("p (a b) c -> p a b c", b=2)[:, :, par, :],
                    in1=t2[:].rearrange("p (a b) c -> p a b c", b=2)[:, :, par, :], op=ALU.add),
                    reads=[t1b, t2b], writes=[krb_b])
            for j in range(4):
                for c in range(2):
                    k.op(PE, lambda e, j=j, c=c: e.transpose(
                        out=pCT[:, c * 512 + j * 128:c * 512 + (j + 1) * 128],
                        in_=ckv[:, j, c * 128:(c + 1) * 128], identity=ident[:]),
                        reads=[ckvb], writes=[pCTb], inc=False)
                k.op(PE, lambda e, j=j: e.transpose(out=pKR[:, j * 128:(j + 1) * 128], in_=krb_t[:, j, :],
                                                    identity=ident[:]),
                     reads=[krb_b], writes=[pKRb], inc=(j == 3))
            k.op(ACT, lambda e: e.activation(out=ckvT[:], in_=pCT[:].rearrange("p (c t) -> p c t", c=2),
                                             func=AF.Copy), reads=[pCTb], writes=[ckvTb])
            krt, krtb = krT[g]
            for par in range(2):
                k.op(DVE, lambda e, blk=blk, krt=krt, par=par: e.tensor_copy(
                    out=krt[par * 64:(par + 1) * 64, blk * 256:(blk + 1) * 256].rearrange("p (a t) -> p a t", a=2),
                    in_=pKR[par * 64:(par + 1) * 64, 0:512].rearrange("p (a b t) -> p a b t", a=2, b=2)[:, :, par, :]),
                    reads=[pKRb], writes=[krtb])
            kst, kstb = Kst[i % 2]
            for h in range(NH):
                p, pb = pO[h % 2], pOb[h % 2]
                for c in range(2):
                    k.op(PE, lambda e, c=c, h=h, p=p: e.matmul(p[:], lhsT=wuk[:, c, h * 128:(h + 1) * 128],
                                                                rhs=ckvT[:, c, :], start=(c == 0), stop=(c == 1)),
                         reads=[ckvTb], writes=[pb], inc=(c == 1))
                evac_copy(kst[:, h, :], p[:], [pb], [kstb])
            for hq in range(4):
                k.dma(SP, SCR[g]["Kd"].rearrange("h d s -> d h s")[:, hq * 4:(hq + 1) * 4, blk * 512:(blk + 1) * 512],
                      kst[:, hq * 4:(hq + 1) * 4, :], kstb, reads=[kstb], store=True, chain=True)
            vst, vstb = Vst[i % 2]
            n = 0
            for j in range(4):
                for hq in range(4):
                    p, pb = pO[n % 2], pOb[n % 2]
                    n += 1
                    for c in range(2):
                        k.op(PE, lambda e, c=c, j=j, hq=hq, p=p: e.matmul(
                            p[:], lhsT=ckvT[:, c, j * 128:(j + 1) * 128], rhs=wuv[:, c, hq * 512:(hq + 1) * 512],
                            start=(c == 0), stop=(c == 1)), reads=[ckvTb], writes=[pb], inc=(c == 1))
                    evac_copy(vst[:, hq * 4:(hq + 1) * 4, j, :], p[:].rearrange("p (h v) -> p h v", h=4), [pb], [vstb])
            for hq in range(4):
                k.dma(SP, SCR[g]["Vd"].rearrange("h p (i v) -> p h i v", v=128)[:, hq * 4:(hq + 1) * 4,
                                                                                 blk * 4:(blk + 1) * 4, :],
                      vst[:, hq * 4:(hq + 1) * 4, :, :], vstb, reads=[vstb], store=True, chain=True)
        nblk = len(blocks)
        issue_load(0)
        if nblk > 1:
            issue_load(1)
        front(0)
        for i in range(nblk):
            if i + 1 < nblk:
                front(i + 1)
            if i + 2 < nblk:
                issue_load(i + 2)
            back(i)
        k.barrier()
        sc.close()

    def phase_q():
        sc = Scope(k)
        wdq = load_weight(sc, "wdq", I["w_dq"].rearrange("(c p) n -> p c n", p=128), [128, 8, 384], [PE])
        wuq = load_weight(sc, "wuq", I["w_uq"].rearrange("(c p) n -> p c n", p=128), [128, 3, NH * 384], [PE])
        g_attn = load_gain(sc, "g_attn", G_ATTN0, D, [DVE])
        g_q = load_gain(sc, "g_q", G_QN, 384, [DVE])
        nb = norm_scope(sc)
        st, stb = nb[2], nb[3]
        xs_t = [sc.sb("xw%d" % i, [128, 4, D], F32, dma=True) for i in range(2)]
        rq_t = [sc.sb("rq%d" % i, [128, 2, 512], F32, dma=True) for i in range(3)]
        hT_t = [sc.sb("hT%d" % i, [128, 8, 512], BF16) for i in range(2)]
        cqa, cqab = sc.sb("cqa", [128, 4, 384], F32)
        cqn, cqnb = sc.sb("cqn", [128, 4, 384], BF16)
        cqT, cqTb = sc.sb("cqT", [128, 3, 512], BF16)
        u1, u1b = sc.sb("u1", [128, 512], F32)
        u2, u2b = sc.sb("u2", [128, 512], F32)
        Qsn = [sc.sb("Qsn%d" % i, [128, NH, 512], BF16, dma=True) for i in range(1)]
        Qsr = [sc.sb("Qsr%d" % i, [128, NH, 512], BF16, dma=True) for i in range(1)]
        pH = [sc.ps("pH%d" % i, [128, 1024], BF16) for i in range(2)]
        pHb = [Buf("pH%d" % i) for i in range(2)]
        pC = [sc.ps("pC%d" % i, [128, 512], F32) for i in range(1)]
        pCb = [Buf("pC%d" % i) for i in range(1)]
        pT = sc.ps("pT", [128, 2048], BF16)
        pTb = Buf("pT")
        pN = sc.ps("pN", [128, 512], F32)
        pNb = Buf("pN")
        pR = sc.ps("pR", [128, 512], F32)
        pRb = Buf("pR")
        pS = sc.ps("pS", [128, 512], F32)
        pSb = Buf("pS")

        blocks = []
        for g, cfg in enumerate(GROUPS):
            for blk in range(cfg["W"] // 512):
                blocks.append((g, blk))

        def issue_load(i):
            g, blk = blocks[i]
            xt, xb = xs_t[i % 2]
            src = I["xw%d" % g].rearrange("(t p) d -> p t d", p=128)
            for j in range(4):
                k.dma(SP, xt[:, j, :], src[:, blk * 4 + j, :], xb, writes=[xb], chain=True)
            rt, rb = rq_t[i % 3]
            k.dma(SP, rt[:], I["ropeq%d" % g][:, :, blk * 512:(blk + 1) * 512], rb, writes=[rb])

        def front(i):
            xt, xb = xs_t[i % 2]
            hT, hTb = hT_t[i % 2]
            norm_block(nb, xt, xb, 4, g_attn, hT, hTb, pH, pHb)

        def back(i):
            g, blk = blocks[i]
            rt, rb = rq_t[i % 3]
            hT, hTb = hT_t[i % 2]
            for j in range(4):
                p, pb = pC[0], pCb[0]
                for c in range(8):
                    k.op(PE, lambda e, c=c, j=j, p=p: e.matmul(p[:, 0:384], lhsT=hT[:, c, j * 128:(j + 1) * 128],
                                                                rhs=wdq[:, c, :], start=(c == 0), stop=(c == 7)),
                         reads=[hTb], writes=[pb], inc=(c == 7))
                k.op(ACT, lambda e, j=j, p=p: e.activation(out=nb[0][:, 0:384], in_=p[:, 0:384], func=AF.Square,
                                                           accum_out=st[:, 32 + j:33 + j]),
                     reads=[pb], writes=[nb[1], stb])
                k.op(DVE, lambda e, j=j, p=p: e.tensor_copy(out=cqa[:, j, :], in_=p[:, 0:384]),
                     reads=[pb], writes=[cqab])
            rstd_small(st, stb, 4, 384, 32)
            for j in range(4):
                k.op(DVE, lambda e, j=j: e.scalar_tensor_tensor(
                    out=cqn[:, j, :], in0=cqa[:, j, :], scalar=st[:, 56 + j:57 + j], in1=g_q[:],
                    op0=ALU.mult, op1=ALU.mult), reads=[cqab, stb], writes=[cqnb])
            for c in range(3):
                for j in range(4):
                    k.op(PE, lambda e, j=j, c=c: e.transpose(
                        out=pT[:, c * 512 + j * 128:c * 512 + (j + 1) * 128],
                        in_=cqn[:, j, c * 128:(c + 1) * 128], identity=ident[:]),
                        reads=[cqnb], writes=[pTb], inc=(c == 2 and j == 3))
            k.op(ACT, lambda e: e.activation(out=cqT[:], in_=pT[:, 0:1536].rearrange("p (c t) -> p c t", c=3),
                                             func=AF.Copy), reads=[pTb], writes=[cqTb])
            qsn, qsnb = Qsn[0]
            qsr, qsrb = Qsr[0]
            for h in range(NH):
                for c in range(3):
                    k.op(PE, lambda e, c=c, h=h: e.matmul(pN[:], lhsT=wuq[:, c, h * 384:h * 384 + 128],
                                                          rhs=cqT[:, c, :], start=(c == 0), stop=(c == 2)),
                         reads=[cqTb], writes=[pNb], inc=(c == 2))
                k.op(ACT, lambda e, h=h: e.activation(out=qsn[:, h, :], in_=pN[:], func=AF.Copy),
                     reads=[pNb], writes=[qsnb])
                for c in range(3):
                    k.op(PE, lambda e, c=c, h=h: e.matmul(pR[:], lhsT=wuq[:, c, h * 384 + 128:h * 384 + 256],
                                                          rhs=cqT[:, c, :], start=(c == 0), stop=(c == 2)),
                         reads=[cqTb], writes=[pRb], inc=(c == 2))
                for c in range(3):
                    k.op(PE, lambda e, c=c, h=h: e.matmul(pS[:], lhsT=wuq[:, c, h * 384 + 256:h * 384 + 384],
                                                          rhs=cqT[:, c, :], start=(c == 0), stop=(c == 2)),
                         reads=[cqTb], writes=[pSb], inc=(c == 2))
                k.op(DVE, lambda e: e.tensor_tensor(out=u1[:], in0=pR[:], in1=rt[:, 0, :], op=ALU.mult),
                     reads=[pRb, rb], writes=[u1b])
                k.op(DVE, lambda e: e.tensor_tensor(out=u2[:], in0=pS[:], in1=rt[:, 1, :], op=ALU.mult),
                     reads=[pSb, rb], writes=[u2b])
                k.op(DVE, lambda e, h=h: e.tensor_tensor(out=qsr[:, h, :], in0=u1[:], in1=u2[:], op=ALU.add),
                     reads=[u1b, u2b], writes=[qsrb])
            for hq in range(4):
                k.dma(SP, SCR[g]["Qn"].rearrange("h d w -> d h w")[:, hq * 4:(hq + 1) * 4, blk * 512:(blk + 1) * 512],
                      qsn[:, hq * 4:(hq + 1) * 4, :], qsnb, reads=[qsnb], store=True, chain=True)
            for hq in range(4):
                k.dma(SP, SCR[g]["Qr"].rearrange("h d w -> d h w")[:, hq * 4:(hq + 1) * 4, blk * 512:(blk + 1) * 512],
                      qsr[:, hq * 4:(hq + 1) * 4, :], qsrb, reads=[qsrb], store=True, chain=True)
        nblk = len(blocks)
        issue_load(0)
        if nblk > 1:
            issue_load(1)
        front(0)
        for i in range(nblk):
            if i + 1 < nblk:
                front(i + 1)
            if i + 2 < nblk:
                issue_load(i + 2)
            back(i)
        k.barrier()
        sc.close()

    def phase_att():
        scale = float(192 ** -0.5)
        sc = Scope(k)
        SMAX = GROUPS[0]["S"]
        Kt = [sc.sb("Kt%d" % i, [128, SMAX], BF16, dma=True) for i in range(2)]
        Vt = [sc.sb("Vt%d" % i, [128, SMAX // 128, 128], BF16, dma=True) for i in range(2)]
        Qn_t = [sc.sb("qn%d" % i, [128, 512], BF16, dma=True) for i in range(2)]
        Qr_t = [sc.sb("qr%d" % i, [128, 512], BF16, dma=True) for i in range(2)]
        PT = [sc.sb("PT%d" % i, [128, 2, 512], BF16) for i in range(3)]
        accD = [sc.sb("accD%d" % i, [128, 512], F32) for i in range(2)]
        accP = [sc.sb("accP%d" % i, [128, 512], F32) for i in range(2)]
        asum = [sc.sb("asum%d" % i, [128, 512], F32) for i in range(2)]
        rden, rdenb = sc.sb("rden", [128, 512], F32)
        Ost = [sc.sb("Ost%d" % i, [128, 512], BF16, dma=True) for i in range(2)]
        pS = [sc.ps("pS%d" % i, [128, 2, 512], F32) for i in range(3)]
        pSb = [Buf("pS%d" % i) for i in range(3)]
        pO = [sc.ps("pO%d" % i, [128, 512], F32) for i in range(2)]
        pOb = [Buf("pO%d" % i) for i in range(2)]

        heads = []
        for g, cfg in enumerate(GROUPS):
            for h in range(NH):
                heads.append((g, h))
        qblocks = []
        for hi, (g, h) in enumerate(heads):
            for qb in range(GROUPS[g]["W"] // 512):
                qblocks.append((hi, g, h, qb))

        def load_head(hi):
            g, h = heads[hi]
            S, NT = GROUPS[g]["S"], GROUPS[g]["NT"]
            kt, ktb = Kt[hi % 2]
            vt, vtb = Vt[hi % 2]
            nparts = max(1, S // 4096)
            for pz in range(nparts):
                c0, c1 = pz * (S // nparts), (pz + 1) * (S // nparts)
                k.dma(SP, kt[:, c0:c1], SCR[g]["Kd"][h, :, c0:c1], ktb, writes=[ktb], chain=True)
            for pz in range(nparts):
                i0_, i1_ = pz * (NT // nparts), (pz + 1) * (NT // nparts)
                k.dma(SP, vt[:, i0_:i1_, :],
                      SCR[g]["Vd"][h].rearrange("p (i v) -> p i v", v=128)[:, i0_:i1_, :], vtb, writes=[vtb],
                      chain=True)

        def load_q(qi):
            hi, g, h, qb = qblocks[qi]
            qn, qnb = Qn_t[qi % 2]
            qr, qrb = Qr_t[qi % 2]
            k.dma(SP, qn[:], SCR[g]["Qn"][h, :, qb * 512:(qb + 1) * 512], qnb, writes=[qnb])
            k.dma(SP, qr[:], SCR[g]["Qr"][h, :, qb * 512:(qb + 1) * 512], qrb, writes=[qrb])

        groups = []
        for qi, (hi, g, h, qb) in enumerate(qblocks):
            ng = GROUPS[g]["NT"] // 2
            for kg in range(ng):
                groups.append((qi, kg, ng))
        slotc = [0]
        slot_of = {}

        def next_slot():
            sl = slotc[0] % 3
            slotc[0] += 1
            return sl

        def emit_qk(gi):
            qi, kg, ng = groups[gi]
            hi, g, h, qb = qblocks[qi]
            if kg == 0:
                if qi == 0:
                    load_head(0)
                    load_q(0)
                if qi + 1 < len(qblocks):
                    load_q(qi + 1)
            if kg == 2 and qb == 0 and hi + 1 < len(heads):
                load_head(hi + 1)
            kt, ktb = Kt[hi % 2]
            qn, qnb = Qn_t[qi % 2]
            qr, qrb = Qr_t[qi % 2]
            krt, krtb = krT[g]
            sl = next_slot()
            slot_of[gi] = sl
            ps, psb = pS[sl], pSb[sl]
            for c in range(2):
                kc = kg * 2 + c
                k.op(PE, lambda e, c=c, kc=kc: e.matmul(ps[:, c, :], lhsT=kt[:, kc * 128:(kc + 1) * 128], rhs=qn[:],
                                                         start=True, stop=False),
                     reads=[ktb, qnb], writes=[psb], inc=False)
            for c in range(2):
                k.op(PE, lambda e, c=c: e.matmul(ps[:, c, :], lhsT=krt[c * 64:(c + 1) * 64, kg * 128:(kg + 1) * 128],
                                                 rhs=qr[c * 64:(c + 1) * 64, :], start=False, stop=True),
                     reads=[krtb, qrb], writes=[psb], inc=(c == 1))
            pt, ptb = PT[gi % 3]
            k.op(ACT, lambda e: e.activation(out=pt[:], in_=ps[:], func=AF.Exp, scale=scale),
                 reads=[psb], writes=[ptb])
            for c, (E_, accl) in enumerate(((DVE, accD), (POOL, accP))):
                acc, accb = accl[qi % 2]
                if kg == 0:
                    k.op(E_, lambda e, c=c, acc=acc: e.tensor_copy(out=acc[:], in_=pt[:, c, :]),
                         reads=[ptb], writes=[accb])
                else:
                    k.op(E_, lambda e, c=c, acc=acc: e.tensor_tensor(out=acc[:], in0=acc[:], in1=pt[:, c, :],
                                                                      op=ALU.add),
                         reads=[ptb, accb], writes=[accb])
            if kg == ng - 1:
                asm, asmb = asum[qi % 2]
                k.op(DVE, lambda e: e.tensor_tensor(out=asm[:], in0=accD[qi % 2][0][:], in1=accP[qi % 2][0][:],
                                                    op=ALU.add),
                     reads=[accD[qi % 2][1], accP[qi % 2][1]], writes=[asmb])

        def emit_pv(gi):
            qi, kg, ng = groups[gi]
            hi, g, h, qb = qblocks[qi]
            vt, vtb = Vt[hi % 2]
            pt, ptb = PT[gi % 3]
            po, pob = pO[qi % 2], pOb[qi % 2]
            for c in range(2):
                kc = kg * 2 + c
                last = (kg == ng - 1 and c == 1)
                k.op(PE, lambda e, c=c, kc=kc, last=last: e.matmul(po[:], lhsT=vt[:, kc, :], rhs=pt[:, c, :],
                                                                    start=(kc == 0), stop=last),
                     reads=[vtb, ptb], writes=[pob], inc=(c == 1))

        def emit_den(qi):
            hi, g, h, qb = qblocks[qi]
            asm, asmb = asum[qi % 2]
            po, pob = pO[qi % 2], pOb[qi % 2]
            sl = next_slot()
            ps, psb = pS[sl], pSb[sl]
            k.op(PE, lambda e: e.matmul(ps[:, 0, :], lhsT=onesf[:], rhs=asm[:], start=True, stop=True),
                 reads=[asmb], writes=[psb], inc=True)
            k.op(DVE, lambda e: e.reciprocal(out=rden[:], in_=ps[:, 0, :]), reads=[psb], writes=[rdenb])
            ost, ostb = Ost[qi % 2]
            k.op(DVE, lambda e: e.tensor_tensor(out=ost[:], in0=po[:], in1=rden[:], op=ALU.mult),
                 reads=[pob, rdenb], writes=[ostb])
            k.dma(SP, SCR[g]["Od"][h, :, qb * 512:(qb + 1) * 512], ost[:], ostb, reads=[ostb], store=True)

        n = len(groups)
        emit_qk(0)
        if n > 1:
            emit_qk(1)
        for gi in range(n):
            emit_pv(gi)
            if gi + 2 < n:
                emit_qk(gi + 2)
            if gi >= 1:
                qi_p, kg_p, ng_p = groups[gi - 1]
                if kg_p == ng_p - 1:
                    emit_den(qi_p)
        emit_den(groups[n - 1][0])
        k.barrier()
        sc.close()

    def phase_wo(src_key, w_in, kchunks, kp, xin_fn, xout_key):
        sc = Scope(k)
        wo = load_weight(sc, "wo", w_in.rearrange("(c p) n -> p c n", p=kp), [kp, kchunks, D], [PE])
        Oin = [sc.sb("Oin%d" % i, [kp, kchunks, 512], BF16, dma=True) for i in range(2)]
        xs_t = [sc.sb("xr%d" % i, [128, 4, D], F32, dma=True) for i in range(2)]
        pA = [sc.ps("pA%d" % i, [128, 512], F32) for i in range(4)]
        pAb = [Buf("pA%d" % i) for i in range(4)]
        blocks = []
        for g, cfg in enumerate(GROUPS):
            for blk in range(cfg["W"] // 512):
                blocks.append((g, blk))

        def issue_load(i):
            g, blk = blocks[i]
            xt, xb = xs_t[i % 2]
            src = xin_fn(g).rearrange("(t p) d -> p t d", p=128)
            for j in range(4):
                k.dma(SP, xt[:, j, :], src[:, blk * 4 + j, :], xb, writes=[xb], chain=True)
            ot, ob = Oin[i % 2]
            srcO = SCR[g][src_key]
            if srcO.shape[1] != kp:
                srcO = srcO.rearrange("(f hh) d w -> f (hh d) w", hh=kp // srcO.shape[1])
            for hq in range(kchunks // 4):
                k.dma(SP, ot[:, hq * 4:(hq + 1) * 4, :],
                      srcO.rearrange("h d w -> d h w")[:, hq * 4:(hq + 1) * 4, blk * 512:(blk + 1) * 512],
                      ob, writes=[ob], chain=True)

        issue_load(0)
        n = 0
        for i, (g, blk) in enumerate(blocks):
            if i + 1 < len(blocks):
                issue_load(i + 1)
            xt, xb = xs_t[i % 2]
            ot, ob = Oin[i % 2]
            for j in range(4):
                for half in range(2):
                    p, pb = pA[n % 4], pAb[n % 4]
                    n += 1
                    for c in range(kchunks):
                        k.op(PE, lambda e, c=c, j=j, half=half, p=p: e.matmul(
                            p[:], lhsT=ot[:, c, j * 128:(j + 1) * 128], rhs=wo[:, c, half * 512:(half + 1) * 512],
                            start=(c == 0), stop=(c == kchunks - 1)),
                            reads=[ob], writes=[pb], inc=(c == kchunks - 1))
                    k.op(DVE, lambda e, j=j, half=half, p=p: e.tensor_tensor(
                        out=xt[:, j, half * 512:(half + 1) * 512], in0=p[:], in1=xt[:, j, half * 512:(half + 1) * 512],
                        op=ALU.add), reads=[pb, xb], writes=[xb])
            dst = SCR[g][xout_key].rearrange("(t p) d -> p t d", p=128)
            k.dma(SP, dst[:, blk * 4:(blk + 1) * 4, :], xt[:], xb, reads=[xb], store=True)
        k.barrier()
        sc.close()

    def phase_mlp(layer, xin_key, xout_key, final):
        sc = Scope(k)
        w1 = load_weight(sc, "w1", I["w1"][layer].rearrange("(c p) n -> p c n", p=128), [128, 8, DFF], [PE])
        w2 = load_weight(sc, "w2", I["w2"][layer].rearrange("(c p) n -> p c n", p=128), [128, 32, D], [PE])
        g_mlp = load_gain(sc, "g_mlp", G_MLP0 if layer == 0 else G_MLP1, D, [DVE])
        g_fin = load_gain(sc, "g_fin", G_FIN, D, [DVE]) if final else None
        nb = norm_scope(sc)
        st, stb = nb[2], nb[3]
        xs_t = [sc.sb("xm%d" % i, [128, 2, D], F32, dma=True) for i in range(3)]
        hT_t = [sc.sb("hT%d" % i, [128, 8, 256], BF16) for i in range(2)]
        hff, hffb = sc.sb("hff", [128, 32, 256], BF16)
        rl = [sc.sb("rl%d" % i, [128, 256], F32) for i in range(2)]
        yo = [sc.sb("yo%d" % i, [128, 2, D], F32, dma=True) for i in range(1)] if final else None
        pH = [sc.ps("pH%d" % i, [128, 1024], BF16) for i in range(2)]
        pHb = [Buf("pH%d" % i) for i in range(2)]
        pF = [sc.ps("pF%d" % i, [128, 512], F32) for i in range(3)]
        pFb = [Buf("pF%d" % i) for i in range(3)]
        pY = [sc.ps("pY%d" % i, [128, 512], F32) for i in range(3)]
        pYb = [Buf("pY%d" % i) for i in range(3)]
        blocks = []
        for g, cfg in enumerate(GROUPS):
            for blk in range(cfg["W"] // 256):
                blocks.append((g, blk))

        def issue_load(i):
            g, blk = blocks[i]
            xt, xb = xs_t[i % 3]
            src = SCR[g][xin_key].rearrange("(t p) d -> p t d", p=128)
            for j in range(2):
                k.dma(SP, xt[:, j, :], src[:, blk * 2 + j, :], xb, writes=[xb], chain=True)

        ctr = [0, 0]

        def front(i):
            xt, xb = xs_t[i % 3]
            hT, hTb = hT_t[i % 2]
            norm_block(nb, xt, xb, 2, g_mlp, hT, hTb, pH, pHb)

        def back(i):
            g, blk = blocks[i]
            xt, xb = xs_t[i % 3]
            hT, hTb = hT_t[i % 2]
            for fc in range(32):
                p, pb = pF[ctr[0] % 3], pFb[ctr[0] % 3]
                r, rb = rl[ctr[0] % 2]
                ctr[0] += 1
                for c in range(8):
                    k.op(PE, lambda e, c=c, fc=fc, p=p: e.matmul(p[:, 0:256], lhsT=w1[:, c, fc * 128:(fc + 1) * 128],
                                                                  rhs=hT[:, c, :], start=(c == 0), stop=(c == 7)),
                         reads=[hTb], writes=[pb], inc=(c == 7))
                k.op(ACT, lambda e, p=p, r=r: e.activation(out=r[:], in_=p[:, 0:256], func=AF.Relu),
                     reads=[pb], writes=[rb])
                k.op(DVE, lambda e, fc=fc, r=r: e.tensor_tensor(out=hff[:, fc, :], in0=r[:], in1=r[:], op=ALU.mult),
                     reads=[rb], writes=[hffb])
            for j in range(2):
                for half in range(2):
                    p, pb = pY[ctr[1] % 3], pYb[ctr[1] % 3]
                    ctr[1] += 1
                    for fc in range(32):
                        k.op(PE, lambda e, fc=fc, j=j, half=half, p=p: e.matmul(
                            p[:], lhsT=hff[:, fc, j * 128:(j + 1) * 128], rhs=w2[:, fc, half * 512:(half + 1) * 512],
                            start=(fc == 0), stop=(fc == 31)), reads=[hffb], writes=[pb], inc=(fc == 31))
                    k.op(DVE, lambda e, j=j, half=half, p=p: e.tensor_tensor(
                        out=xt[:, j, half * 512:(half + 1) * 512], in0=p[:], in1=xt[:, j, half * 512:(half + 1) * 512],
                        op=ALU.add), reads=[pb, xb], writes=[xb])
            if not final:
                dst = SCR[g][xout_key].rearrange("(t p) d -> p t d", p=128)
                k.dma(SP, dst[:, blk * 2:(blk + 1) * 2, :], xt[:], xb, reads=[xb], store=True)
            else:
                yt, yb = yo[0]
                junk, junkb = nb[0], nb[1]
                for j in range(2):
                    k.op(ACT, lambda e, j=j: e.activation(out=junk[:], in_=xt[:, j, :], func=AF.Square,
                                                          accum_out=st[:, 40 + j:41 + j]),
                         reads=[xb], writes=[junkb, stb])
                rstd_small(st, stb, 2, D, 40)
                for j in range(2):
                    k.op(DVE, lambda e, j=j: e.scalar_tensor_tensor(
                        out=yt[:, j, :], in0=xt[:, j, :], scalar=st[:, 64 + j:65 + j],
                        in1=g_fin[:], op0=ALU.mult, op1=ALU.mult), reads=[xb, stb], writes=[yb])
                dst = OUT[g].rearrange("(t p) d -> p t d", p=128)
                k.dma(SP, dst[:, blk * 2:(blk + 1) * 2, :], yt[:], yb, reads=[yb], store=True)
        nblk = len(blocks)
        issue_load(0)
        if nblk > 1:
            issue_load(1)
        front(0)
        for i in range(nblk):
            if i + 1 < nblk:
                front(i + 1)
            if i + 2 < nblk:
                issue_load(i + 2)
            back(i)
        k.barrier()
        sc.close()

    def phase_naprep():
        sc = Scope(k)
        wqkv = load_weight(sc, "wqkv", I["na_qkv"].rearrange("(c p) n -> p c n", p=128), [128, 8, 3 * D], [PE])
        g_attn = load_gain(sc, "g_attn1", G_ATTN1, D, [DVE])
        nb = norm_scope(sc)
        xs_t = [sc.sb("xn%d" % i, [128, 4, D], F32, dma=True) for i in range(2)]
        hT_t = [sc.sb("hT%d" % i, [128, 8, 512], BF16) for i in range(2)]
        Fst = [sc.sb("Fst%d" % i, [128, 24, 512], BF16, dma=True) for i in range(2)]
        pH = [sc.ps("pH%d" % i, [128, 1024], BF16) for i in range(2)]
        pHb = [Buf("pH%d" % i) for i in range(2)]
        pA = [sc.ps("pA%d" % i, [128, 512], F32) for i in range(4)]
        pAb = [Buf("pA%d" % i) for i in range(4)]
        blocks = []
        for g, cfg in enumerate(GROUPS):
            for blk in range(cfg["W"] // 512):
                blocks.append((g, blk))

        def issue_load(i):
            g, blk = blocks[i]
            xt, xb = xs_t[i % 2]
            src = SCR[g]["X2"].rearrange("(t p) d -> p t d", p=128)
            for j in range(4):
                k.dma(SP, xt[:, j, :], src[:, blk * 4 + j, :], xb, writes=[xb], chain=True)

        nctr = [0]

        def front(i):
            xt, xb = xs_t[i % 2]
            hT, hTb = hT_t[i % 2]
            norm_block(nb, xt, xb, 4, g_attn, hT, hTb, pH, pHb)

        def back(i):
            g, blk = blocks[i]
            hT, hTb = hT_t[i % 2]
            fst, fstb = Fst[i % 2]
            for f in range(24):
                p, pb = pA[nctr[0] % 4], pAb[nctr[0] % 4]
                nctr[0] += 1
                for c in range(8):
                    k.op(PE, lambda e, c=c, f=f, p=p: e.matmul(p[:], lhsT=wqkv[:, c, f * 128:(f + 1) * 128],
                                                                rhs=hT[:, c, :], start=(c == 0), stop=(c == 7)),
                         reads=[hTb], writes=[pb], inc=(c == 7))
                if f % 2 == 0:
                    k.op(ACT, lambda e, f=f, p=p: e.activation(out=fst[:, f, :], in_=p[:], func=AF.Copy),
                         reads=[pb], writes=[fstb])
                else:
                    k.op(DVE, lambda e, f=f, p=p: e.tensor_copy(out=fst[:, f, :], in_=p[:]),
                         reads=[pb], writes=[fstb])
            for qi, key in enumerate(("NQ", "NK", "NV")):
                for hq in range(2):
                    k.dma(SP, SCR[g][key].rearrange("f d w -> d f w")[:, hq * 4:(hq + 1) * 4, blk * 512:(blk + 1) * 512],
                          fst[:, qi * 8 + hq * 4:qi * 8 + (hq + 1) * 4, :], fstb, reads=[fstb], store=True, chain=True)
        nblk = len(blocks)
        issue_load(0)
        if nblk > 1:
            issue_load(1)
        front(0)
        for i in range(nblk):
            if i + 1 < nblk:
                front(i + 1)
            if i + 2 < nblk:
                issue_load(i + 2)
            back(i)
        k.barrier()
        sc.close()

    def phase_na():
        scale = float(64 ** -0.5)
        sc = Scope(k)
        WMAX = GROUPS[0]["W"]
        RMAX = GROUPS[0]["R"]
        qT = [[sc.sb("nqT%d_%d" % (i, hh), [64, WMAX], BF16, dma=True) for hh in range(2)] for i in range(2)]
        kT = [[sc.sb("nkT%d_%d" % (i, hh), [64, WMAX], BF16, dma=True) for hh in range(2)] for i in range(2)]
        vT = [sc.sb("nvT%d" % i, [128, WMAX], BF16, dma=True) for i in range(2)]
        tb = [sc.sb("ntb%d" % i, [128, 14, 2, 64], F32, dma=True) for i in range(2)]
        mask, maskb = sc.sb("nmask", [128, 64], F32, dma=True)
        Vall, Vallb = sc.sb("Vall", [128, RMAX - 1, 128], BF16)
        sbt = [sc.sb("nsb%d" % i, [128, 4, 2, 64], F32) for i in range(3)]
        ptt = [sc.sb("xpt%d" % i, [128, 4, 2, 64], BF16) for i in range(3)]
        itc = [0]
        rd, rdb = sc.sb("nrd", [64, 2, 64], F32)
        Ost = [sc.sb("nOst%d" % i, [64, 2, WMAX], BF16, dma=True) for i in range(2)]
        pV = [sc.ps("npV%d" % i, [128, 1024], BF16) for i in range(2)]
        pVb = [Buf("npV%d" % i) for i in range(2)]
        pS = [sc.ps("npS%d" % i, [128, 512], F32) for i in range(3)]
        pSb = [Buf("npS%d" % i) for i in range(3)]
        pO = [sc.ps("npO%d" % i, [128, 512], F32) for i in range(3)]
        pOb = [Buf("npO%d" % i) for i in range(3)]
        tm = k.dma(SP, mask[:], I["namask"][:, :], maskb, writes=[maskb])
        DVE.wait(tm)

        pairs = []
        for g, cfg in enumerate(GROUPS):
            for f in range(8):
                pairs.append((g, f))

        def load_pair(pi):
            g, f = pairs[pi]
            W = GROUPS[g]["W"]
            for (tt, key) in ((qT, "NQ"), (kT, "NK")):
                for hh in range(2):
                    t, b = tt[pi % 2][hh]
                    k.dma(SP, t[:, 0:W], SCR[g][key][f, hh * 64:(hh + 1) * 64, :], b, writes=[b])
            t, b = vT[pi % 2]
            k.dma(SP, t[:, 0:W], SCR[g]["NV"][f, :, :], b, writes=[b])
            t, b = tb[pi % 2]
            k.dma(SP, t[:], I["nab"][:, f], b, writes=[b])

        load_pair(0)
        it = 0
        for pi, (g, f) in enumerate(pairs):
            if pi + 1 < len(pairs):
                load_pair(pi + 1)
            R, W = GROUPS[g]["R"], GROUPS[g]["W"]
            qh = qT[pi % 2]
            kh = kT[pi % 2]
            v, vb_ = vT[pi % 2]
            tab, tabb = tb[pi % 2]
            for hh in range(2):
                for rr in range(14):
                    k.op(DVE, lambda e, hh=hh, rr=rr: e.tensor_tensor(
                        out=tab[:, rr, hh, :], in0=tab[:, rr, hh, :], in1=mask[:], op=ALU.add),
                        reads=[tabb, maskb], writes=[tabb])
            a = 0
            nv = 0
            while a < R - 1:
                cnt = min(8, R - 1 - a)
                p, pb = pV[nv % 2], pVb[nv % 2]
                nv += 1
                for u in range(cnt):
                    k.op(PE, lambda e, u=u, a=a, p=p: e.transpose(
                        out=p[:, u * 128:(u + 1) * 128], in_=v[:, (a + u) * 64:(a + u) * 64 + 128], identity=ident[:]),
                        reads=[vb_], writes=[pb], inc=(u == cnt - 1))
                k.op(ACT, lambda e, a=a, cnt=cnt, p=p: e.activation(
                    out=Vall[:, a:a + cnt, :], in_=p[:, 0:cnt * 128].rearrange("p (u v) -> p u v", v=128),
                    func=AF.Copy), reads=[pb], writes=[Vallb])
                a += cnt
            its = list(range(R))
            st_ = {}
            ost, ostb = Ost[pi % 2]

            def emit_s(ii):
                r = its[ii]
                r0 = min(max(r - 4, 0), R - 8)
                rr0 = 7 - (r - r0)
                n_ = itc[0]
                itc[0] += 1
                ps, psb = pS[n_ % 3], pSb[n_ % 3]
                sbx, sbxb = sbt[n_ % 3]
                ptx, ptxb = ptt[n_ % 3]
                st_[ii] = (r0, ptx, ptxb, n_)
                for hh in range(2):
                    hb = hh * 64
                    for j in range(4):
                        k.op(PE, lambda e, j=j, hh=hh, hb=hb: e.matmul(
                            ps[:, j * 128 + hh * 64:j * 128 + (hh + 1) * 64],
                            lhsT=kh[hh][0][:, (r0 + 2 * j) * 64:(r0 + 2 * j) * 64 + 128],
                            rhs=qh[hh][0][:, r * 64:(r + 1) * 64], start=True, stop=True),
                            reads=[kh[hh][1], qh[hh][1]], writes=[psb], inc=(hh == 1 and j == 3))
                k.op(DVE, lambda e: e.scalar_tensor_tensor(
                    out=sbx[:].rearrange("p j h c -> p j (h c)"), in0=ps[:, 0:512].rearrange("p (j x) -> p j x", j=4),
                    scalar=scale, in1=tab[:, rr0:rr0 + 7:2, :, :].rearrange("p j h c -> p j (h c)"),
                    op0=ALU.mult, op1=ALU.add),
                    reads=[psb, tabb], writes=[sbxb])
                k.op(ACT, lambda e: e.activation(out=ptx[:], in_=sbx[:], func=AF.Exp),
                     reads=[sbxb], writes=[ptxb])

            def emit_o(ii):
                r = its[ii]
                r0, ptx, ptxb, n_ = st_[ii]
                po, pob = pO[n_ % 3], pOb[n_ % 3]
                for hh in range(2):
                    hb = hh * 64
                    for j in range(4):
                        k.op(PE, lambda e, j=j, hh=hh, hb=hb: e.matmul(
                            po[0:64, hh * 64:(hh + 1) * 64], lhsT=Vall[:, r0 + 2 * j, hb:hb + 64],
                            rhs=ptx[:, j, hh, :], start=(j == 0), stop=(j == 3)),
                            reads=[Vallb, ptxb], writes=[pob], inc=False)
                for j in range(4):
                    k.op(PE, lambda e, j=j: e.matmul(
                        po[0:64, 128:256], lhsT=onesb[:, 0:64], rhs=ptx[:, j, :, :].rearrange("p h c -> p (h c)"),
                        start=(j == 0), stop=(j == 3)), reads=[ptxb], writes=[pob], inc=(j == 3))
                k.op(DVE, lambda e: e.reciprocal(out=rd[:].rearrange("p h c -> p (h c)"), in_=po[0:64, 128:256]),
                     reads=[pob], writes=[rdb])
                k.op(DVE, lambda e: e.tensor_tensor(
                    out=ost[:, :, r * 64:(r + 1) * 64], in0=po[0:64, 0:128].rearrange("p (h c) -> p h c", h=2),
                    in1=rd[:], op=ALU.mult), reads=[pob, rdb], writes=[ostb])
                if r == R - 1:
                    k.dma(SP, SCR[g]["NO"].rearrange("h d w -> d h w")[:, 2 * f:2 * f + 2, :], ost[:, :, 0:W], ostb,
                          reads=[ostb], store=True)

            ni = len(its)
            emit_s(0)
            emit_s(1)
            for ii in range(ni):
                emit_o(ii)
                if ii + 2 < ni:
                    emit_s(ii + 2)
        k.barrier()
        sc.close()

    NP = NPHASES[0]
    if NP >= 1:
        phase_kv()
    else:
        k.barrier()
    if NP >= 2:
        phase_q()
    if NP >= 3:
        phase_att()
    krsc.close()
    if NP >= 4:
        phase_wo("Od", I["w_o"], 16, 128, lambda g: I["xw%d" % g], "X1")
    if NP >= 5:
        phase_mlp(0, "X1", "X2", False)
    if NP >= 6:
        phase_naprep()
    if NP >= 7:
        phase_na()
    if NP >= 8:
        phase_wo("NO", I["na_o"], 8, 128, lambda g: SCR[g]["X2"], "X3")
    if NP >= 9:
        phase_mlp(1, "X3", None, True)
    glob.close()
    es.close()
    return nc


def _rope_tables(S):
    inv = (10000.0 ** (-np.arange(0, 64, 2, dtype=np.float32) / np.float32(64))).astype(np.float32)
    ang = (np.arange(S, dtype=np.float32)[:, None] * inv[None, :]).astype(np.float32)
    return np.cos(ang).astype(np.float32), np.sin(ang).astype(np.float32)


def _win_start(g, idx):
    cfg = GROUPS[g]
    rows = cfg["S"] // GRID_W
    own0 = idx * cfg["OWN"]
    ws = min(max(own0 - 4, 0), rows - cfg["R"])
    return ws, own0 - ws


_NC_CACHE = {}


def _prepare(x_prompt, x_sample, attn_norm, mlp_norm, final_norm, mla_w_dq, mla_q_norm, mla_w_uq,
             mla_w_dkv, mla_kv_norm, mla_w_ukv, mla_w_o, na_w_qkv, na_rpb, na_w_o, mlp_w1, mlp_w2):
    f32 = np.float32
    x_prompt = np.asarray(x_prompt, f32)
    x_sample = np.asarray(x_sample, f32)
    xs = [x_prompt, x_sample]
    ncores = 8
    gains = np.zeros((GW,), f32)
    gains[G_ATTN0:G_ATTN0 + D] = np.asarray(attn_norm)[0]
    gains[G_MLP0:G_MLP0 + D] = np.asarray(mlp_norm)[0]
    gains[G_ATTN1:G_ATTN1 + D] = np.asarray(attn_norm)[1]
    gains[G_MLP1:G_MLP1 + D] = np.asarray(mlp_norm)[1]
    gains[G_FIN:G_FIN + D] = np.asarray(final_norm)
    gains[G_QN:G_QN + 384] = np.asarray(mla_q_norm)[0]
    gains[G_KVN:G_KVN + 256] = np.asarray(mla_kv_norm)[0]
    gains = np.ascontiguousarray(np.broadcast_to(gains[None, :], (128, GW)))
    wuq = np.asarray(mla_w_uq, f32)[0].reshape(384, NH, 192)
    sw = np.concatenate([wuq[:, :, 160:192], wuq[:, :, 128:160]], axis=2)
    wuq2 = np.concatenate([wuq[:, :, 0:192], wuq[:, :, 128:192], sw, sw], axis=2)
    wuq2 = np.ascontiguousarray(wuq2.reshape(384, NH * 384))
    wukv = np.asarray(mla_w_ukv, f32)[0].reshape(256, NH, 256)
    w_uk = np.ascontiguousarray(wukv[:, :, 0:128].reshape(256, NH * 128))
    w_uv = np.ascontiguousarray(wukv[:, :, 128:256].reshape(256, NH * 128))
    rpb = np.asarray(na_rpb, f32)[0]
    p = np.arange(128)
    kc = p % 64
    hi = p // 64
    c = np.arange(64)
    rel = kc[:, None] - c[None, :] + 15
    cs = np.clip(c - 8, 0, 48)
    inwin = (kc[:, None] >= cs[None, :]) & (kc[:, None] < cs[None, :] + 16)
    relc = np.clip(rel, 0, 30)
    rr = np.arange(14)
    nab = rpb[:, (rr[None, :, None] + hi[:, None, None]), relc[:, None, :]]
    nab = np.transpose(nab, (1, 0, 2, 3)).reshape(128, 8, 2, 14, 64)
    nab = np.ascontiguousarray(np.transpose(nab, (0, 1, 3, 2, 4))).astype(f32)
    namask = np.where(inwin, 0.0, NEG).astype(f32)
    shared = dict(
        gains=gains, ident=np.eye(128, dtype=f32),
        w_dkv=np.ascontiguousarray(np.asarray(mla_w_dkv, f32)[0]),
        w_dq=np.ascontiguousarray(np.asarray(mla_w_dq, f32)[0]),
        w_uq=wuq2, w_uk=w_uk, w_uv=w_uv,
        w_o=np.ascontiguousarray(np.asarray(mla_w_o, f32)[0]),
        w1=np.ascontiguousarray(np.asarray(mlp_w1, f32)), w2=np.ascontiguousarray(np.asarray(mlp_w2, f32)),
        na_qkv=np.ascontiguousarray(np.asarray(na_w_qkv, f32)[0]),
        na_o=np.ascontiguousarray(np.asarray(na_w_o, f32)[0]),
        nab=nab, namask=namask,
    )
    ropek = []
    rope_full = []
    for g, cfg in enumerate(GROUPS):
        S, NT = cfg["S"], cfg["NT"]
        cos, sin = _rope_tables(S)
        rope_full.append((cos, sin))
        t = np.concatenate([cos, cos, -sin, sin], axis=1)
        ropek.append(np.ascontiguousarray(t.reshape(128, NT, 128)))
    in_maps = []
    meta = []
    for cidx in range(ncores):
        m = dict(shared)
        mm = []
        for g, cfg in enumerate(GROUPS):
            per = ncores // xs[g].shape[0]
            b, idx = cidx // per, cidx % per
            ws, off = _win_start(g, idx)
            t0 = ws * GRID_W
            W = cfg["W"]
            m["xs%d" % g] = np.ascontiguousarray(xs[g][b])
            m["xw%d" % g] = np.ascontiguousarray(xs[g][b, t0:t0 + W])
            m["ropek%d" % g] = ropek[g]
            cos, sin = rope_full[g]
            cw, sw = cos[t0:t0 + W].T, sin[t0:t0 + W].T
            rq = np.stack([np.concatenate([cw, cw, cw, cw], 0), np.concatenate([-sw, sw, -sw, sw], 0)], axis=1)
            m["ropeq%d" % g] = np.ascontiguousarray(rq.astype(f32))
            mm.append((b, idx, off))
        in_maps.append(m)
        meta.append(mm)
    return in_maps, meta, (x_prompt.shape, x_sample.shape)


def kernel(**inputs):
    f32 = np.float32
    in_maps, meta, shapes = _prepare(**inputs)
    ncores = len(in_maps)
    if "nc" not in _NC_CACHE:
        _NC_CACHE["nc"] = build_program()
    nc = _NC_CACHE["nc"]
    res = run_bass_kernel_spmd(nc, in_maps, core_ids=list(range(ncores)))
    outs = [np.zeros(shapes[0], f32), np.zeros(shapes[1], f32)]
    for cidx in range(ncores):
        r = res.results[cidx]
        for g, cfg in enumerate(GROUPS):
            b, idx, off = meta[cidx][g]
            own = cfg["OWN"] * GRID_W
            y = np.asarray(r["y%d" % g])
            outs[g][b, idx * own:(idx + 1) * own] = y[off * GRID_W:off * GRID_W + own]
    return (outs[0], outs[1])
```

```python
import numpy as np
from contextlib import ExitStack
import concourse.bass as bass
import concourse.mybir as mybir
from concourse.bass_utils import run_bass_kernel_spmd

F32 = mybir.dt.float32
BF16 = mybir.dt.bfloat16
AF = mybir.ActivationFunctionType
ALU = mybir.AluOpType

D = 1024
DFF = 4096
NH = 16
EPS = 1e-6
GRID_W = 64
GROUPS = [dict(S=16384, R=72, OWN=64), dict(S=4096, R=40, OWN=32)]


def _fix_groups():
    for _g in GROUPS:
        _g["W"] = _g["R"] * GRID_W
        _g["NT"] = _g["S"] // 128


_fix_groups()
NPHASES = [9]
import os
KVSTOP = int(os.environ.get("KVSTOP", "99"))
G_ATTN0, G_MLP0, G_ATTN1, G_MLP1, G_FIN, G_QN, G_KVN = 0, 1024, 2048, 3072, 4096, 5120, 5504
GW = 5760
NEG = -30000.0


class Sem:
    ALL = []

    def __init__(self, h):
        self.h = h
        self.cnt = 0
        self.uid = len(Sem.ALL)
        Sem.ALL.append(self)


class Tok:
    __slots__ = ("sem", "val")

    def __init__(self, sem, val):
        self.sem = sem
        self.val = val


class Buf:
    def __init__(self, name, sem=None, excl=None):
        self.name = name
        self.w = None
        self.r = {}
        self.sem = sem
        self.excl = name.startswith("p") or name.startswith("np") if excl is None else excl


class Eng:
    def __init__(self, name, e, sem, inorder=False):
        self.name = name
        self.e = e
        self.sem = sem
        self.seen = {}
        self.pending = []
        self.inorder = inorder

    def wait(self, t):
        if t is None:
            return
        if t.sem is self.sem and self.inorder:
            return
        if t.val is None:
            raise RuntimeError("wait on unresolved token (%s)" % self.name)
        k = t.sem.uid
        if self.seen.get(k, 0) >= t.val:
            return
        self.e.wait_ge(t.sem.h, t.val)
        self.seen[k] = t.val


class K:
    def __init__(self, nc, es):
        self.nc = nc
        self.es = es
        self.sempool = []
        self.nsem = 0
        mk = self.newsem
        self.PE = Eng("pe", nc.tensor, mk(), inorder=True)
        self.ACT = Eng("act", nc.scalar, mk())
        self.DVE = Eng("dve", nc.vector, mk())
        self.POOL = Eng("pool", nc.gpsimd, mk())
        self.SP = Eng("sp", nc.sync, mk())
        self.engs = [self.PE, self.ACT, self.DVE, self.POOL, self.SP]
        self.bar = mk()
        self.dma_out = []

    def newsem(self):
        self.nsem += 1
        h = self.es.enter_context(self.nc.semaphore("s%d" % self.nsem))
        return Sem(h)

    def getsem(self):
        if self.sempool:
            return self.sempool.pop()
        return self.newsem()

    def putsem(self, s):
        self.sempool.append(s)

    def _deps(self, E, reads, writes, skipsem=None):
        for b in reads:
            if b.w is not None and b.w.sem is not skipsem:
                E.wait(b.w)
            if b.excl:
                for t in b.r.values():
                    if t.sem is not E.sem:
                        E.wait(t)
        for b in writes:
            if b.w is not None and b.w.sem is not skipsem:
                E.wait(b.w)
            for t in b.r.values():
                E.wait(t)

    def _post(self, tok, reads, writes):
        for b in reads:
            b.r[tok.sem.uid] = tok
        for b in writes:
            b.w = tok
            b.r = {}

    def op(self, E, fn, reads=(), writes=(), inc=True):
        self._deps(E, reads, writes)
        ins = fn(E.e)
        tok = Tok(E.sem, None)
        if inc:
            E.sem.cnt += 1
            ins.then_inc(E.sem.h, 1)
            tok.val = E.sem.cnt
            for p in E.pending:
                p.val = E.sem.cnt
            E.pending = []
        else:
            E.pending.append(tok)
        self._post(tok, reads, writes)
        return tok

    def dma(self, Q, out, in_, sembuf, reads=(), writes=(), chain=False, store=False):
        sem = sembuf.sem
        self._deps(Q, reads, writes, skipsem=sem if chain else None)
        ins = Q.e.dma_start(out=out, in_=in_)
        sem.cnt += 16
        ins.then_inc(sem.h, 16)
        tok = Tok(sem, sem.cnt)
        self._post(tok, reads, writes)
        if store:
            self.dma_out.append(tok)
        return tok

    def barrier(self):
        SP = self.SP
        for E in self.engs:
            if E is SP:
                continue
            if E.pending:
                raise RuntimeError("pending tokens at barrier on " + E.name)
            if E.sem.cnt > 0:
                SP.wait(Tok(E.sem, E.sem.cnt))
        last = {}
        for t in self.dma_out:
            if t.sem.uid not in last or last[t.sem.uid].val < t.val:
                last[t.sem.uid] = t
        for t in last.values():
            SP.wait(t)
        self.dma_out = []
        self.bar.cnt += 1
        SP.e.sem_inc(self.bar.h, 1)
        for E in self.engs:
            if E is SP:
                continue
            E.e.wait_ge(self.bar.h, self.bar.cnt)


class Scope:
    CNT = 0

    def __init__(self, k):
        self.k = k
        self.es = ExitStack()
        self.sems = []
        self.n = 0
        Scope.CNT += 1
        self.uid = Scope.CNT

    def sb(self, name, shape, dt, dma=False, fresh=False):
        self.n += 1
        t = self.es.enter_context(self.k.nc.sbuf_tensor("%s_%d" % (name, self.uid), shape, dt))
        b = Buf(name)
        if dma and fresh:
            b.sem = self.k.newsem()
        elif dma:
            b.sem = self.k.getsem()
            self.sems.append(b.sem)
        return t, b

    def ps(self, name, shape, dt):
        t = self.es.enter_context(self.k.nc.psum_tensor("%s_%d" % (name, self.uid), shape, dt))
        return t

    def close(self):
        for s in self.sems:
            self.k.putsem(s)
        self.es.close()


def build_program():
    nc = bass.Bass("TRN2", target_bir_lowering=False)
    es = ExitStack()

    def din(name, shape, dt=F32):
        return nc.dram_tensor(name, list(shape), dt, kind="ExternalInput").ap()

    def dscr(name, shape, dt):
        return nc.dram_tensor(name, list(shape), dt, kind="Internal").ap()

    def dout(name, shape, dt=F32):
        return nc.dram_tensor(name, list(shape), dt, kind="ExternalOutput").ap()

    I = {}
    for g, cfg in enumerate(GROUPS):
        S, W, NT = cfg["S"], cfg["W"], cfg["NT"]
        I["xs%d" % g] = din("xs%d" % g, [S, D])
        I["xw%d" % g] = din("xw%d" % g, [W, D])
        I["ropek%d" % g] = din("ropek%d" % g, [128, NT, 128])
        I["ropeq%d" % g] = din("ropeq%d" % g, [128, 2, W])
    I["gains"] = din("gains", [128, GW])
    I["ident"] = din("ident", [128, 128])
    I["w_dkv"] = din("w_dkv", [D, 320])
    I["w_dq"] = din("w_dq", [D, 384])
    I["w_uq"] = din("w_uq", [384, NH * 384])
    I["w_uk"] = din("w_uk", [256, NH * 128])
    I["w_uv"] = din("w_uv", [256, NH * 128])
    I["w_o"] = din("w_o", [2048, D])
    I["w1"] = din("w1", [2, D, DFF])
    I["w2"] = din("w2", [2, DFF, D])
    I["na_qkv"] = din("na_qkv", [D, 3 * D])
    I["na_o"] = din("na_o", [D, D])
    I["nab"] = din("nab", [128, 8, 14, 2, 64])
    I["namask"] = din("namask", [128, 64])

    SCR = []
    OUT = []
    for g, cfg in enumerate(GROUPS):
        S, W, NT = cfg["S"], cfg["W"], cfg["NT"]
        SCR.append(dict(
            Kd=dscr("Kd%d" % g, [NH, 128, S], BF16),
            Vd=dscr("Vd%d" % g, [NH, 128, NT * 128], BF16),
            Qn=dscr("Qn%d" % g, [NH, 128, W], BF16),
            Qr=dscr("Qr%d" % g, [NH, 128, W], BF16),
            Od=dscr("Od%d" % g, [NH, 128, W], BF16),
            X1=dscr("X1%d" % g, [W, D], F32),
            X2=dscr("X2%d" % g, [W, D], F32),
            X3=dscr("X3%d" % g, [W, D], F32),
            NQ=dscr("NQ%d" % g, [8, 128, W], BF16),
            NK=dscr("NK%d" % g, [8, 128, W], BF16),
            NV=dscr("NV%d" % g, [8, 128, W], BF16),
            NO=dscr("NO%d" % g, [NH, 64, W], BF16),
        ))
        OUT.append(dout("y%d" % g, [W, D]))

    k = K(nc, es)
    PE, ACT, DVE, POOL, SP = k.PE, k.ACT, k.DVE, k.POOL, k.SP
    es.enter_context(nc.Block())

    glob = Scope(k)
    ident, identb = glob.sb("ident", [128, 128], BF16, dma=True, fresh=True)
    onesf, onesfb = glob.sb("onesf", [128, 128], F32)
    onesb, onesbb = glob.sb("onesb", [128, 64], BF16)
    krsc = Scope(k)
    krT = []
    for g, cfg in enumerate(GROUPS):
        krT.append(krsc.sb("krT%d" % g, [128, cfg["S"] // 2], BF16))
    t_id = k.dma(POOL, ident[:], I["ident"][:, :], identb, writes=[identb])
    t_of = k.op(POOL, lambda e: e.memset(onesf[:], 1.0), writes=[onesfb])
    t_ob = k.op(POOL, lambda e: e.memset(onesb[:], 1.0), writes=[onesbb])
    for E in (PE,):
        E.wait(t_id)
        E.wait(t_of)
        E.wait(t_ob)

    def alloc_weight(sc, name, shape):
        return sc.sb(name, shape, BF16, dma=True, fresh=True)

    def issue_weight(t, b, src3, nchunk):
        tok = None
        for c in range(nchunk):
            tok = k.dma(POOL, t[:, c], src3[:, c], b, writes=[b], chain=True)
        return tok

    def load_weight(sc, name, src3, shape, engines):
        t, b = alloc_weight(sc, name, shape)
        tok = issue_weight(t, b, src3, shape[1])
        for E in engines:
            E.wait(tok)
        return t

    def load_gain(sc, name, off, n, engines):
        t, b = sc.sb(name, [128, n], F32, dma=True)
        tok = k.dma(SP, t[:], I["gains"][:, off:off + n], b, writes=[b])
        for E in engines:
            E.wait(tok)
        return t

    def norm_block(sc_bufs, x, xb, T, gain, hT, hTb, pH, pHb):
        junk, junkb, st, stb, hb_t, hb_b = sc_bufs
        for j in range(T):
            k.op(ACT, lambda e, j=j: e.activation(out=junk[:], in_=x[:, j, :], func=AF.Square,
                                                  accum_out=st[:, j:j + 1]),
                 reads=[xb], writes=[junkb, stb])
        k.op(DVE, lambda e: e.tensor_scalar(out=st[:, 8:8 + T], in0=st[:, 0:T], scalar1=1.0 / D, scalar2=EPS,
                                            op0=ALU.mult, op1=ALU.add), reads=[], writes=[stb])
        k.op(ACT, lambda e: e.activation(out=st[:, 16:16 + T], in_=st[:, 8:8 + T], func=AF.Sqrt),
             reads=[], writes=[stb])
        k.op(DVE, lambda e: e.reciprocal(out=st[:, 24:24 + T], in_=st[:, 16:16 + T]), reads=[], writes=[stb])
        for j in range(T):
            hj, hjb = hb_t[j % 2], hb_b[j % 2]
            k.op(DVE, lambda e, j=j, hj=hj: e.scalar_tensor_tensor(
                out=hj[:], in0=x[:, j, :], scalar=st[:, 24 + j:25 + j], in1=gain[:],
                op0=ALU.mult, op1=ALU.mult), reads=[xb, stb], writes=[hjb])
            p, pb = pH[j % 2], pHb[j % 2]
            for c in range(8):
                k.op(PE, lambda e, c=c, hj=hj, p=p: e.transpose(out=p[:, c * 128:(c + 1) * 128],
                                                                 in_=hj[:, c * 128:(c + 1) * 128], identity=ident[:]),
                     reads=[hjb], writes=[pb], inc=(c == 7))
            k.op(ACT, lambda e, j=j, p=p: e.activation(
                out=hT[:, :, j * 128:(j + 1) * 128], in_=p[:].rearrange("p (c t) -> p c t", c=8), func=AF.Copy),
                reads=[pb], writes=[hTb])

    def rstd_small(st, stb, T, n, base):
        k.op(DVE, lambda e: e.tensor_scalar(out=st[:, base + 8:base + 8 + T], in0=st[:, base:base + T],
                                            scalar1=1.0 / n, scalar2=EPS, op0=ALU.mult, op1=ALU.add),
             reads=[], writes=[stb])
        k.op(ACT, lambda e: e.activation(out=st[:, base + 16:base + 16 + T], in_=st[:, base + 8:base + 8 + T],
                                         func=AF.Sqrt), reads=[], writes=[stb])
        k.op(DVE, lambda e: e.reciprocal(out=st[:, base + 24:base + 24 + T], in_=st[:, base + 16:base + 16 + T]),
             reads=[], writes=[stb])

    def norm_scope(sc):
        junk, junkb = sc.sb("junk", [128, D], BF16)
        st, stb = sc.sb("st", [128, 96], F32)
        h0, h0b = sc.sb("h0", [128, D], BF16)
        h1, h1b = sc.sb("h1", [128, D], BF16)
        return (junk, junkb, st, stb, [h0, h1], [h0b, h1b])

    def phase_kv():
        sc = Scope(k)
        wdkv = load_weight(sc, "wdkv", I["w_dkv"].rearrange("(c p) n -> p c n", p=128), [128, 8, 320], [PE])
        wuk = load_weight(sc, "wuk", I["w_uk"].rearrange("(c p) n -> p c n", p=128), [128, 2, 2048], [PE])
        wuv = load_weight(sc, "wuv", I["w_uv"].rearrange("(c p) n -> p c n", p=128), [128, 2, 2048], [PE])
        g_attn = load_gain(sc, "g_attn", G_ATTN0, D, [DVE])
        g_kv = load_gain(sc, "g_kv", G_KVN, 256, [DVE])
        nb = norm_scope(sc)
        st, stb = nb[2], nb[3]
        xs_t = [sc.sb("xs%d" % i, [128, 4, D], F32, dma=True) for i in range(2)]
        rp_t = [sc.sb("rp%d" % i, [128, 4, 128], F32, dma=True) for i in range(3)]
        hT_t = [sc.sb("hT%d" % i, [128, 8, 512], BF16) for i in range(2)]
        kva, kvab = sc.sb("kva", [128, 4, 320], F32)
        ckv, ckvb = sc.sb("ckv", [128, 4, 256], BF16)
        t1, t1b = sc.sb("t1", [128, 4, 64], F32)
        t2, t2b = sc.sb("t2", [128, 4, 64], F32)
        krb_t, krb_b = sc.sb("krb", [128, 4, 128], BF16)
        k.op(POOL, lambda e: e.memset(krb_t[:], 0.0), writes=[krb_b])
        ckvT, ckvTb = sc.sb("ckvT", [128, 2, 512], BF16)
        Kst = [sc.sb("Kst%d" % i, [128, NH, 512], BF16, dma=True) for i in range(2)]
        Vst = [sc.sb("Vst%d" % i, [128, NH, 4, 128], BF16, dma=True) for i in range(2)]
        pH = [sc.ps("pH%d" % i, [128, 1024], BF16) for i in range(2)]
        pHb = [Buf("pH%d" % i) for i in range(2)]
        pKV = [sc.ps("pKV%d" % i, [128, 512], F32) for i in range(2)]
        pKVb = [Buf("pKV%d" % i) for i in range(2)]
        pCT = sc.ps("pCT", [128, 1024], BF16)
        pCTb = Buf("pCT")
        pKR = sc.ps("pKR", [128, 1024], BF16)
        pKRb = Buf("pKR")
        pO = [sc.ps("pO%d" % i, [128, 512], F32) for i in range(2)]
        pOb = [Buf("pO%d" % i) for i in range(2)]

        blocks = []
        for g, cfg in enumerate(GROUPS):
            for blk in range(cfg["S"] // 512):
                blocks.append((g, blk))

        def issue_load(i):
            g, blk = blocks[i]
            NT = GROUPS[g]["NT"]
            xt, xb = xs_t[i % 2]
            src = I["xs%d" % g].rearrange("(p n) d -> p n d", n=NT)
            for j in range(4):
                k.dma(SP, xt[:, j, :], src[:, blk * 4 + j, :], xb, writes=[xb], chain=True)
            rt, rb = rp_t[i % 3]
            k.dma(SP, rt[:], I["ropek%d" % g][:, blk * 4:blk * 4 + 4, :], rb, writes=[rb])

        evac = [0]

        def evac_copy(out, in_, reads, writes):
            evac[0] += 1
            if evac[0] % 2 == 0:
                return k.op(ACT, lambda e: e.activation(out=out, in_=in_, func=AF.Copy), reads=reads, writes=writes)
            return k.op(DVE, lambda e: e.tensor_copy(out=out, in_=in_), reads=reads, writes=writes)

        def front(i):
            xt, xb = xs_t[i % 2]
            hT, hTb = hT_t[i % 2]
            norm_block(nb, xt, xb, 4, g_attn, hT, hTb, pH, pHb)

        def back(i):
            g, blk = blocks[i]
            S, NT = GROUPS[g]["S"], GROUPS[g]["NT"]
            rt, rb = rp_t[i % 3]
            hT, hTb = hT_t[i % 2]
            for j in range(4):
                p, pb = pKV[j % 2], pKVb[j % 2]
                for c in range(8):
                    k.op(PE, lambda e, c=c, j=j, p=p: e.matmul(p[:, 0:320], lhsT=hT[:, c, j * 128:(j + 1) * 128],
                                                                rhs=wdkv[:, c, :], start=(c == 0), stop=(c == 7)),
                         reads=[hTb], writes=[pb], inc=(c == 7))
                k.op(ACT, lambda e, j=j, p=p: e.activation(out=nb[0][:, 0:256], in_=p[:, 0:256], func=AF.Square,
                                                           accum_out=st[:, 32 + j:33 + j]),
                     reads=[pb], writes=[nb[1], stb])
                k.op(DVE, lambda e, j=j, p=p: e.tensor_copy(out=kva[:, j, :], in_=p[:, 0:320]),
                     reads=[pb], writes=[kvab])
            rstd_small(st, stb, 4, 256, 32)
            for j in range(4):
                k.op(DVE, lambda e, j=j: e.scalar_tensor_tensor(
                    out=ckv[:, j, :], in0=kva[:, j, 0:256], scalar=st[:, 56 + j:57 + j], in1=g_kv[:],
                    op0=ALU.mult, op1=ALU.mult), reads=[kvab, stb], writes=[ckvb])
            k.op(DVE, lambda e: e.tensor_tensor(out=t1[:], in0=kva[:, :, 256:320], in1=rt[:, :, 0:64], op=ALU.mult),
                 reads=[kvab, rb], writes=[t1b])
            k.op(DVE, lambda e: e.tensor_tensor(out=t2[:, :, 0:32], in0=kva[:, :, 288:320], in1=rt[:, :, 64:96],
                                                op=ALU.mult), reads=[kvab, rb], writes=[t2b])
            k.op(DVE, lambda e: e.tensor_tensor(out=t2[:, :, 32:64], in0=kva[:, :, 256:288], in1=rt[:, :, 96:128],
                                                op=ALU.mult), reads=[kvab, rb], writes=[t2b])
            for par in range(2):
                k.op(DVE, lambda e, par=par: e.tensor_tensor(
                    out=krb_t[:].rearrange("p (a b) c -> p a b c", b=2)[:, :, par, par * 64:(par + 1) * 64],
                    in0=t1[:].rearrange("p (a b) c -> p a b c", b=2)[:, :, par, :],
                    in1=t2[:].rearrange("p (a b) c -> p a b c", b=2)[:, :, par, :], op=ALU.add),
                    reads=[t1b, t2b], writes=[krb_b])
            for j in range(4):
                for c in range(2):
                    k.op(PE, lambda e, j=j, c=c: e.transpose(
                        out=pCT[:, c * 512 + j * 128:c * 512 + (j + 1) * 128],
                        in_=ckv[:, j, c * 128:(c + 1) * 128], identity=ident[:]),
                        reads=[ckvb], writes=[pCTb], inc=False)
                k.op(PE, lambda e, j=j: e.transpose(out=pKR[:, j * 128:(j + 1) * 128], in_=krb_t[:, j, :],
                                                    identity=ident[:]),
                     reads=[krb_b], writes=[pKRb], inc=(j == 3))
            k.op(ACT, lambda e: e.activation(out=ckvT[:], in_=pCT[:].rearrange("p (c t) -> p c t", c=2),
                                             func=AF.Copy), reads=[pCTb], writes=[ckvTb])
            krt, krtb = krT[g]
            for par in range(2):
                k.op(DVE, lambda e, blk=blk, krt=krt, par=par: e.tensor_copy(
                    out=krt[par * 64:(par + 1) * 64, blk * 256:(blk + 1) * 256].rearrange("p (a t) -> p a t", a=2),
                    in_=pKR[par * 64:(par + 1) * 64, 0:512].rearrange("p (a b t) -> p a b t", a=2, b=2)[:, :, par, :]),
                    reads=[pKRb], writes=[krtb])
            kst, kstb = Kst[i % 2]
            for h in range(NH):
                p, pb = pO[h % 2], pOb[h % 2]
                for c in range(2):
                    k.op(PE, lambda e, c=c, h=h, p=p: e.matmul(p[:], lhsT=wuk[:, c, h * 128:(h + 1) * 128],
                                                                rhs=ckvT[:, c, :], start=(c == 0), stop=(c == 1)),
                         reads=[ckvTb], writes=[pb], inc=(c == 1))
                evac_copy(kst[:, h, :], p[:], [pb], [kstb])
            for hq in range(4):
                k.dma(SP, SCR[g]["Kd"].rearrange("h d s -> d h s")[:, hq * 4:(hq + 1) * 4, blk * 512:(blk + 1) * 512],
                      kst[:, hq * 4:(hq + 1) * 4, :], kstb, reads=[kstb], store=True, chain=True)
            vst, vstb = Vst[i % 2]
            n = 0
            for j in range(4):
                for hq in range(4):
                    p, pb = pO[n % 2], pOb[n % 2]
                    n += 1
                    for c in range(2):
                        k.op(PE, lambda e, c=c, j=j, hq=hq, p=p: e.matmul(
                            p[:], lhsT=ckvT[:, c, j * 128:(j + 1) * 128], rhs=wuv[:, c, hq * 512:(hq + 1) * 512],
                            start=(c == 0), stop=(c == 1)), reads=[ckvTb], writes=[pb], inc=(c == 1))
                    evac_copy(vst[:, hq * 4:(hq + 1) * 4, j, :], p[:].rearrange("p (h v) -> p h v", h=4), [pb], [vstb])
            for hq in range(4):
                k.dma(SP, SCR[g]["Vd"].rearrange("h p (i v) -> p h i v", v=128)[:, hq * 4:(hq + 1) * 4,
                                                                                 blk * 4:(blk + 1) * 4, :],
                      vst[:, hq * 4:(hq + 1) * 4, :, :], vstb, reads=[vstb], store=True, chain=True)
        nblk = len(blocks)
        issue_load(0)
        if nblk > 1:
            issue_load(1)
        front(0)
        for i in range(nblk):
            if i + 1 < nblk:
                front(i + 1)
            if i + 2 < nblk:
                issue_load(i + 2)
            back(i)
        k.barrier()
        sc.close()

    def phase_q():
        sc = Scope(k)
        wdq = load_weight(sc, "wdq", I["w_dq"].rearrange("(c p) n -> p c n", p=128), [128, 8, 384], [PE])
        wuq = load_weight(sc, "wuq", I["w_uq"].rearrange("(c p) n -> p c n", p=128), [128, 3, NH * 384], [PE])
        g_attn = load_gain(sc, "g_attn", G_ATTN0, D, [DVE])
        g_q = load_gain(sc, "g_q", G_QN, 384, [DVE])
        nb = norm_scope(sc)
        st, stb = nb[2], nb[3]
        xs_t = [sc.sb("xw%d" % i, [128, 4, D], F32, dma=True) for i in range(2)]
        rq_t = [sc.sb("rq%d" % i, [128, 2, 512], F32, dma=True) for i in range(3)]
        hT_t = [sc.sb("hT%d" % i, [128, 8, 512], BF16) for i in range(2)]
        cqa, cqab = sc.sb("cqa", [128, 4, 384], F32)
        cqn, cqnb = sc.sb("cqn", [128, 4, 384], BF16)
        cqT, cqTb = sc.sb("cqT", [128, 3, 512], BF16)
        u1, u1b = sc.sb("u1", [128, 512], F32)
        u2, u2b = sc.sb("u2", [128, 512], F32)
        Qsn = [sc.sb("Qsn%d" % i, [128, NH, 512], BF16, dma=True) for i in range(1)]
        Qsr = [sc.sb("Qsr%d" % i, [128, NH, 512], BF16, dma=True) for i in range(1)]
        pH = [sc.ps("pH%d" % i, [128, 1024], BF16) for i in range(2)]
        pHb = [Buf("pH%d" % i) for i in range(2)]
        pC = [sc.ps("pC%d" % i, [128, 512], F32) for i in range(1)]
        pCb = [Buf("pC%d" % i) for i in range(1)]
        pT = sc.ps("pT", [128, 2048], BF16)
        pTb = Buf("pT")
        pN = sc.ps("pN", [128, 512], F32)
        pNb = Buf("pN")
        pR = sc.ps("pR", [128, 512], F32)
        pRb = Buf("pR")
        pS = sc.ps("pS", [128, 512], F32)
        pSb = Buf("pS")

        blocks = []
        for g, cfg in enumerate(GROUPS):
            for blk in range(cfg["W"] // 512):
                blocks.append((g, blk))

        def issue_load(i):
            g, blk = blocks[i]
            xt, xb = xs_t[i % 2]
            src = I["xw%d" % g].rearrange("(t p) d -> p t d", p=128)
            for j in range(4):
                k.dma(SP, xt[:, j, :], src[:, blk * 4 + j, :], xb, writes=[xb], chain=True)
            rt, rb = rq_t[i % 3]
            k.dma(SP, rt[:], I["ropeq%d" % g][:, :, blk * 512:(blk + 1) * 512], rb, writes=[rb])

        def front(i):
            xt, xb = xs_t[i % 2]
            hT, hTb = hT_t[i % 2]
            norm_block(nb, xt, xb, 4, g_attn, hT, hTb, pH, pHb)

        def back(i):
            g, blk = blocks[i]
            rt, rb = rq_t[i % 3]
            hT, hTb = hT_t[i % 2]
            for j in range(4):
                p, pb = pC[0], pCb[0]
                for c in range(8):
                    k.op(PE, lambda e, c=c, j=j, p=p: e.matmul(p[:, 0:384], lhsT=hT[:, c, j * 128:(j + 1) * 128],
                                                                rhs=wdq[:, c, :], start=(c == 0), stop=(c == 7)),
                         reads=[hTb], writes=[pb], inc=(c == 7))
                k.op(ACT, lambda e, j=j, p=p: e.activation(out=nb[0][:, 0:384], in_=p[:, 0:384], func=AF.Square,
                                                           accum_out=st[:, 32 + j:33 + j]),
                     reads=[pb], writes=[nb[1], stb])
                k.op(DVE, lambda e, j=j, p=p: e.tensor_copy(out=cqa[:, j, :], in_=p[:, 0:384]),
                     reads=[pb], writes=[cqab])
            rstd_small(st, stb, 4, 384, 32)
            for j in range(4):
                k.op(DVE, lambda e, j=j: e.scalar_tensor_tensor(
                    out=cqn[:, j, :], in0=cqa[:, j, :], scalar=st[:, 56 + j:57 + j], in1=g_q[:],
                    op0=ALU.mult, op1=ALU.mult), reads=[cqab, stb], writes=[cqnb])
            for c in range(3):
                for j in range(4):
                    k.op(PE, lambda e, j=j, c=c: e.transpose(
                        out=pT[:, c * 512 + j * 128:c * 512 + (j + 1) * 128],
                        in_=cqn[:, j, c * 128:(c + 1) * 128], identity=ident[:]),
                        reads=[cqnb], writes=[pTb], inc=(c == 2 and j == 3))
            k.op(ACT, lambda e: e.activation(out=cqT[:], in_=pT[:, 0:1536].rearrange("p (c t) -> p c t", c=3),
                                             func=AF.Copy), reads=[pTb], writes=[cqTb])
            qsn, qsnb = Qsn[0]
            qsr, qsrb = Qsr[0]
            for h in range(NH):
                for c in range(3):
                    k.op(PE, lambda e, c=c, h=h: e.matmul(pN[:], lhsT=wuq[:, c, h * 384:h * 384 + 128],
                                                          rhs=cqT[:, c, :], start=(c == 0), stop=(c == 2)),
                         reads=[cqTb], writes=[pNb], inc=(c == 2))
                k.op(ACT, lambda e, h=h: e.activation(out=qsn[:, h, :], in_=pN[:], func=AF.Copy),
                     reads=[pNb], writes=[qsnb])
                for c in range(3):
                    k.op(PE, lambda e, c=c, h=h: e.matmul(pR[:], lhsT=wuq[:, c, h * 384 + 128:h * 384 + 256],
                                                          rhs=cqT[:, c, :], start=(c == 0), stop=(c == 2)),
                         reads=[cqTb], writes=[pRb], inc=(c == 2))
                for c in range(3):
                    k.op(PE, lambda e, c=c, h=h: e.matmul(pS[:], lhsT=wuq[:, c, h * 384 + 256:h * 384 + 384],
                                                          rhs=cqT[:, c, :], start=(c == 0), stop=(c == 2)),
                         reads=[cqTb], writes=[pSb], inc=(c == 2))
                k.op(DVE, lambda e: e.tensor_tensor(out=u1[:], in0=pR[:], in1=rt[:, 0, :], op=ALU.mult),
                     reads=[pRb, rb], writes=[u1b])
                k.op(DVE, lambda e: e.tensor_tensor(out=u2[:], in0=pS[:], in1=rt[:, 1, :], op=ALU.mult),
                     reads=[pSb, rb], writes=[u2b])
                k.op(DVE, lambda e, h=h: e.tensor_tensor(out=qsr[:, h, :], in0=u1[:], in1=u2[:], op=ALU.add),
                     reads=[u1b, u2b], writes=[qsrb])
            for hq in range(4):
                k.dma(SP, SCR[g]["Qn"].rearrange("h d w -> d h w")[:, hq * 4:(hq + 1) * 4, blk * 512:(blk + 1) * 512],
                      qsn[:, hq * 4:(hq + 1) * 4, :], qsnb, reads=[qsnb], store=True, chain=True)
            for hq in range(4):
                k.dma(SP, SCR[g]["Qr"].rearrange("h d w -> d h w")[:, hq * 4:(hq + 1) * 4, blk * 512:(blk + 1) * 512],
                      qsr[:, hq * 4:(hq + 1) * 4, :], qsrb, reads=[qsrb], store=True, chain=True)
        nblk = len(blocks)
        issue_load(0)
        if nblk > 1:
            issue_load(1)
        front(0)
        for i in range(nblk):
            if i + 1 < nblk:
                front(i + 1)
            if i + 2 < nblk:
                issue_load(i + 2)
            back(i)
        k.barrier()
        sc.close()

    def phase_att():
        scale = float(192 ** -0.5)
        sc = Scope(k)
        SMAX = GROUPS[0]["S"]
        Kt = [sc.sb("Kt%d" % i, [128, SMAX], BF16, dma=True) for i in range(2)]
        Vt = [sc.sb("Vt%d" % i, [128, SMAX // 128, 128], BF16, dma=True) for i in range(2)]
        Qn_t = [sc.sb("qn%d" % i, [128, 512], BF16, dma=True) for i in range(2)]
        Qr_t = [sc.sb("qr%d" % i, [128, 512], BF16, dma=True) for i in range(2)]
        PT = [sc.sb("PT%d" % i, [128, 2, 512], BF16) for i in range(3)]
        accs = [sc.sb("acc%d" % i, [128, 2, 512], F32) for i in range(2)]
        asum = [sc.sb("asum%d" % i, [128, 512], F32) for i in range(2)]
        rden, rdenb = sc.sb("rden", [128, 512], F32)
        Ost = [sc.sb("Ost%d" % i, [128, 512], BF16, dma=True) for i in range(2)]
        pS = [sc.ps("pS%d" % i, [128, 2, 512], F32) for i in range(3)]
        pSb = [Buf("pS%d" % i) for i in range(3)]
        pO = [sc.ps("pO%d" % i, [128, 512], F32) for i in range(2)]
        pOb = [Buf("pO%d" % i) for i in range(2)]

        heads = []
        for g, cfg in enumerate(GROUPS):
            for h in range(NH):
                heads.append((g, h))
        qblocks = []
        for hi, (g, h) in enumerate(heads):
            for qb in range(GROUPS[g]["W"] // 512):
                qblocks.append((hi, g, h, qb))

        def load_head(hi):
            g, h = heads[hi]
            S, NT = GROUPS[g]["S"], GROUPS[g]["NT"]
            kt, ktb = Kt[hi % 2]
            vt, vtb = Vt[hi % 2]
            nparts = max(1, S // 4096)
            for pz in range(nparts):
                c0, c1 = pz * (S // nparts), (pz + 1) * (S // nparts)
                k.dma(SP, kt[:, c0:c1], SCR[g]["Kd"][h, :, c0:c1], ktb, writes=[ktb], chain=True)
            for pz in range(nparts):
                i0_, i1_ = pz * (NT // nparts), (pz + 1) * (NT // nparts)
                k.dma(SP, vt[:, i0_:i1_, :],
                      SCR[g]["Vd"][h].rearrange("p (i v) -> p i v", v=128)[:, i0_:i1_, :], vtb, writes=[vtb],
                      chain=True)

        def load_q(qi):
            hi, g, h, qb = qblocks[qi]
            qn, qnb = Qn_t[qi % 2]
            qr, qrb = Qr_t[qi % 2]
            k.dma(SP, qn[:], SCR[g]["Qn"][h, :, qb * 512:(qb + 1) * 512], qnb, writes=[qnb])
            k.dma(SP, qr[:], SCR[g]["Qr"][h, :, qb * 512:(qb + 1) * 512], qrb, writes=[qrb])

        groups = []
        for qi, (hi, g, h, qb) in enumerate(qblocks):
            ng = GROUPS[g]["NT"] // 2
            for kg in range(ng):
                groups.append((qi, kg, ng))
        slotc = [0]
        slot_of = {}

        def next_slot():
            sl = slotc[0] % 3
            slotc[0] += 1
            return sl

        def emit_qk(gi):
            qi, kg, ng = groups[gi]
            hi, g, h, qb = qblocks[qi]
            if kg == 0:
                if qi == 0:
                    load_head(0)
                    load_q(0)
                if qi + 1 < len(qblocks):
                    load_q(qi + 1)
            if kg == 2 and qb == 0 and hi + 1 < len(heads):
                load_head(hi + 1)
            kt, ktb = Kt[hi % 2]
            qn, qnb = Qn_t[qi % 2]
            qr, qrb = Qr_t[qi % 2]
            krt, krtb = krT[g]
            sl = next_slot()
            slot_of[gi] = sl
            ps, psb = pS[sl], pSb[sl]
            for c in range(2):
                kc = kg * 2 + c
                k.op(PE, lambda e, c=c, kc=kc: e.matmul(ps[:, c, :], lhsT=kt[:, kc * 128:(kc + 1) * 128], rhs=qn[:],
                                                         start=True, stop=False),
                     reads=[ktb, qnb], writes=[psb], inc=False)
            for c in range(2):
                k.op(PE, lambda e, c=c: e.matmul(ps[:, c, :], lhsT=krt[c * 64:(c + 1) * 64, kg * 128:(kg + 1) * 128],
                                                 rhs=qr[c * 64:(c + 1) * 64, :], start=False, stop=True),
                     reads=[krtb, qrb], writes=[psb], inc=(c == 1))
            pt, ptb = PT[gi % 3]
            acc, accb = accs[qi % 2]
            k.op(ACT, lambda e: e.activation(out=pt[:], in_=ps[:], func=AF.Exp, scale=scale),
                 reads=[psb], writes=[ptb])
            if kg == 0:
                k.op(DVE, lambda e: e.tensor_copy(out=acc[:], in_=pt[:]), reads=[ptb], writes=[accb])
            else:
                k.op(DVE, lambda e: e.tensor_tensor(out=acc[:], in0=acc[:], in1=pt[:], op=ALU.add),
                     reads=[ptb, accb], writes=[accb])
            if kg == ng - 1:
                asm, asmb = asum[qi % 2]
                k.op(DVE, lambda e: e.tensor_tensor(out=asm[:], in0=acc[:, 0, :], in1=acc[:, 1, :], op=ALU.add),
                     reads=[accb], writes=[asmb])

        def emit_pv(gi):
            qi, kg, ng = groups[gi]
            hi, g, h, qb = qblocks[qi]
            vt, vtb = Vt[hi % 2]
            pt, ptb = PT[gi % 3]
            po, pob = pO[qi % 2], pOb[qi % 2]
            for c in range(2):
                kc = kg * 2 + c
                last = (kg == ng - 1 and c == 1)
                k.op(PE, lambda e, c=c, kc=kc, last=last: e.matmul(po[:], lhsT=vt[:, kc, :], rhs=pt[:, c, :],
                                                                    start=(kc == 0), stop=last),
                     reads=[vtb, ptb], writes=[pob], inc=(c == 1))

        def emit_den(qi):
            hi, g, h, qb = qblocks[qi]
            asm, asmb = asum[qi % 2]
            po, pob = pO[qi % 2], pOb[qi % 2]
            sl = next_slot()
            ps, psb = pS[sl], pSb[sl]
            k.op(PE, lambda e: e.matmul(ps[:, 0, :], lhsT=onesf[:], rhs=asm[:], start=True, stop=True),
                 reads=[asmb], writes=[psb], inc=True)
            k.op(DVE, lambda e: e.reciprocal(out=rden[:], in_=ps[:, 0, :]), reads=[psb], writes=[rdenb])
            ost, ostb = Ost[qi % 2]
            k.op(DVE, lambda e: e.tensor_tensor(out=ost[:], in0=po[:], in1=rden[:], op=ALU.mult),
                 reads=[pob, rdenb], writes=[ostb])
            k.dma(SP, SCR[g]["Od"][h, :, qb * 512:(qb + 1) * 512], ost[:], ostb, reads=[ostb], store=True)

        n = len(groups)
        emit_qk(0)
        if n > 1:
            emit_qk(1)
        for gi in range(n):
            emit_pv(gi)
            if gi + 2 < n:
                emit_qk(gi + 2)
            if gi >= 1:
                qi_p, kg_p, ng_p = groups[gi - 1]
                if kg_p == ng_p - 1:
                    emit_den(qi_p)
        emit_den(groups[n - 1][0])
        k.barrier()
        sc.close()

    def phase_wo(src_key, w_in, kchunks, kp, xin_fn, xout_key, prefetch=None):
        sc = Scope(k)
        wo = load_weight(sc, "wo", w_in.rearrange("(c p) n -> p c n", p=kp), [kp, kchunks, D], [PE])
        if prefetch is not None:
            prefetch()
        Oin = [sc.sb("Oin%d" % i, [kp, kchunks, 512], BF16, dma=True) for i in range(2)]
        xs_t = [sc.sb("xr%d" % i, [128, 4, D], F32, dma=True) for i in range(2)]
        pA = [sc.ps("pA%d" % i, [128, 512], F32) for i in range(4)]
        pAb = [Buf("pA%d" % i) for i in range(4)]
        blocks = []
        for g, cfg in enumerate(GROUPS):
            for blk in range(cfg["W"] // 512):
                blocks.append((g, blk))

        def issue_load(i):
            g, blk = blocks[i]
            xt, xb = xs_t[i % 2]
            src = xin_fn(g).rearrange("(t p) d -> p t d", p=128)
            for j in range(4):
                k.dma(SP, xt[:, j, :], src[:, blk * 4 + j, :], xb, writes=[xb], chain=True)
            ot, ob = Oin[i % 2]
            srcO = SCR[g][src_key]
            if srcO.shape[1] != kp:
                srcO = srcO.rearrange("(f hh) d w -> f (hh d) w", hh=kp // srcO.shape[1])
            for hq in range(kchunks // 4):
                k.dma(SP, ot[:, hq * 4:(hq + 1) * 4, :],
                      srcO.rearrange("h d w -> d h w")[:, hq * 4:(hq + 1) * 4, blk * 512:(blk + 1) * 512],
                      ob, writes=[ob], chain=True)

        issue_load(0)
        n = 0
        for i, (g, blk) in enumerate(blocks):
            if i + 1 < len(blocks):
                issue_load(i + 1)
            xt, xb = xs_t[i % 2]
            ot, ob = Oin[i % 2]
            for j in range(4):
                for half in range(2):
                    p, pb = pA[n % 4], pAb[n % 4]
                    n += 1
                    for c in range(kchunks):
                        k.op(PE, lambda e, c=c, j=j, half=half, p=p: e.matmul(
                            p[:], lhsT=ot[:, c, j * 128:(j + 1) * 128], rhs=wo[:, c, half * 512:(half + 1) * 512],
                            start=(c == 0), stop=(c == kchunks - 1)),
                            reads=[ob], writes=[pb], inc=(c == kchunks - 1))
                    k.op(DVE, lambda e, j=j, half=half, p=p: e.tensor_tensor(
                        out=xt[:, j, half * 512:(half + 1) * 512], in0=p[:], in1=xt[:, j, half * 512:(half + 1) * 512],
                        op=ALU.add), reads=[pb, xb], writes=[xb])
            dst = SCR[g][xout_key].rearrange("(t p) d -> p t d", p=128)
            k.dma(SP, dst[:, blk * 4:(blk + 1) * 4, :], xt[:], xb, reads=[xb], store=True)
        k.barrier()
        sc.close()

    def phase_mlp(layer, xin_key, xout_key, final, pre=None):
        sc = Scope(k)
        if pre is None:
            w1 = load_weight(sc, "w1", I["w1"][layer].rearrange("(c p) n -> p c n", p=128), [128, 8, DFF], [PE])
            w2 = load_weight(sc, "w2", I["w2"][layer].rearrange("(c p) n -> p c n", p=128), [128, 32, D], [PE])
        else:
            w1, w2 = pre["w1"][0], pre["w2"][0]
            PE.wait(pre["tok1"])
            PE.wait(pre["tok2"])
        g_mlp = load_gain(sc, "g_mlp", G_MLP0 if layer == 0 else G_MLP1, D, [DVE])
        g_fin = load_gain(sc, "g_fin", G_FIN, D, [DVE]) if final else None
        nb = norm_scope(sc)
        st, stb = nb[2], nb[3]
        xs_t = [sc.sb("xm%d" % i, [128, 2, D], F32, dma=True) for i in range(3)]
        hT_t = [sc.sb("hT%d" % i, [128, 8, 256], BF16) for i in range(2)]
        hff, hffb = sc.sb("hff", [128, 32, 256], BF16)
        rl = [sc.sb("rl%d" % i, [128, 256], F32) for i in range(2)]
        yo = [sc.sb("yo%d" % i, [128, 2, D], F32, dma=True) for i in range(1)] if final else None
        pH = [sc.ps("pH%d" % i, [128, 1024], BF16) for i in range(2)]
        pHb = [Buf("pH%d" % i) for i in range(2)]
        pF = [sc.ps("pF%d" % i, [128, 512], F32) for i in range(3)]
        pFb = [Buf("pF%d" % i) for i in range(3)]
        pY = [sc.ps("pY%d" % i, [128, 512], F32) for i in range(3)]
        pYb = [Buf("pY%d" % i) for i in range(3)]
        blocks = []
        for g, cfg in enumerate(GROUPS):
            for blk in range(cfg["W"] // 256):
                blocks.append((g, blk))

        def issue_load(i):
            g, blk = blocks[i]
            xt, xb = xs_t[i % 3]
            src = SCR[g][xin_key].rearrange("(t p) d -> p t d", p=128)
            for j in range(2):
                k.dma(SP, xt[:, j, :], src[:, blk * 2 + j, :], xb, writes=[xb], chain=True)

        ctr = [0, 0]

        def front(i):
            xt, xb = xs_t[i % 3]
            hT, hTb = hT_t[i % 2]
            norm_block(nb, xt, xb, 2, g_mlp, hT, hTb, pH, pHb)

        def back(i):
            g, blk = blocks[i]
            xt, xb = xs_t[i % 3]
            hT, hTb = hT_t[i % 2]
            for fc in range(32):
                p, pb = pF[ctr[0] % 3], pFb[ctr[0] % 3]
                r, rb = rl[ctr[0] % 2]
                ctr[0] += 1
                for c in range(8):
                    k.op(PE, lambda e, c=c, fc=fc, p=p: e.matmul(p[:, 0:256], lhsT=w1[:, c, fc * 128:(fc + 1) * 128],
                                                                  rhs=hT[:, c, :], start=(c == 0), stop=(c == 7)),
                         reads=[hTb], writes=[pb], inc=(c == 7))
                k.op(ACT, lambda e, p=p, r=r: e.activation(out=r[:], in_=p[:, 0:256], func=AF.Relu),
                     reads=[pb], writes=[rb])
                k.op(DVE, lambda e, fc=fc, r=r: e.tensor_tensor(out=hff[:, fc, :], in0=r[:], in1=r[:], op=ALU.mult),
                     reads=[rb], writes=[hffb])
            for j in range(2):
                for half in range(2):
                    p, pb = pY[ctr[1] % 3], pYb[ctr[1] % 3]
                    ctr[1] += 1
                    for fc in range(32):
                        k.op(PE, lambda e, fc=fc, j=j, half=half, p=p: e.matmul(
                            p[:], lhsT=hff[:, fc, j * 128:(j + 1) * 128], rhs=w2[:, fc, half * 512:(half + 1) * 512],
                            start=(fc == 0), stop=(fc == 31)), reads=[hffb], writes=[pb], inc=(fc == 31))
                    k.op(DVE, lambda e, j=j, half=half, p=p: e.tensor_tensor(
                        out=xt[:, j, half * 512:(half + 1) * 512], in0=p[:], in1=xt[:, j, half * 512:(half + 1) * 512],
                        op=ALU.add), reads=[pb, xb], writes=[xb])
            if not final:
                dst = SCR[g][xout_key].rearrange("(t p) d -> p t d", p=128)
                k.dma(SP, dst[:, blk * 2:(blk + 1) * 2, :], xt[:], xb, reads=[xb], store=True)
            else:
                yt, yb = yo[0]
                junk, junkb = nb[0], nb[1]
                for j in range(2):
                    k.op(ACT, lambda e, j=j: e.activation(out=junk[:], in_=xt[:, j, :], func=AF.Square,
                                                          accum_out=st[:, 40 + j:41 + j]),
                         reads=[xb], writes=[junkb, stb])
                rstd_small(st, stb, 2, D, 40)
                for j in range(2):
                    k.op(DVE, lambda e, j=j: e.scalar_tensor_tensor(
                        out=yt[:, j, :], in0=xt[:, j, :], scalar=st[:, 64 + j:65 + j],
                        in1=g_fin[:], op0=ALU.mult, op1=ALU.mult), reads=[xb, stb], writes=[yb])
                dst = OUT[g].rearrange("(t p) d -> p t d", p=128)
                k.dma(SP, dst[:, blk * 2:(blk + 1) * 2, :], yt[:], yb, reads=[yb], store=True)
        nblk = len(blocks)
        issue_load(0)
        if nblk > 1:
            issue_load(1)
        front(0)
        for i in range(nblk):
            if i + 1 < nblk:
                front(i + 1)
            if i + 2 < nblk:
                issue_load(i + 2)
            back(i)
        k.barrier()
        sc.close()

    def phase_naprep():
        sc = Scope(k)
        wqkv = load_weight(sc, "wqkv", I["na_qkv"].rearrange("(c p) n -> p c n", p=128), [128, 8, 3 * D], [PE])
        g_attn = load_gain(sc, "g_attn1", G_ATTN1, D, [DVE])
        nb = norm_scope(sc)
        xs_t = [sc.sb("xn%d" % i, [128, 4, D], F32, dma=True) for i in range(2)]
        hT_t = [sc.sb("hT%d" % i, [128, 8, 512], BF16) for i in range(2)]
        Fst = [sc.sb("Fst%d" % i, [128, 24, 512], BF16, dma=True) for i in range(2)]
        pH = [sc.ps("pH%d" % i, [128, 1024], BF16) for i in range(2)]
        pHb = [Buf("pH%d" % i) for i in range(2)]
        pA = [sc.ps("pA%d" % i, [128, 512], F32) for i in range(4)]
        pAb = [Buf("pA%d" % i) for i in range(4)]
        blocks = []
        for g, cfg in enumerate(GROUPS):
            for blk in range(cfg["W"] // 512):
                blocks.append((g, blk))

        def issue_load(i):
            g, blk = blocks[i]
            xt, xb = xs_t[i % 2]
            src = SCR[g]["X2"].rearrange("(t p) d -> p t d", p=128)
            for j in range(4):
                k.dma(SP, xt[:, j, :], src[:, blk * 4 + j, :], xb, writes=[xb], chain=True)

        nctr = [0]

        def front(i):
            xt, xb = xs_t[i % 2]
            hT, hTb = hT_t[i % 2]
            norm_block(nb, xt, xb, 4, g_attn, hT, hTb, pH, pHb)

        def back(i):
            g, blk = blocks[i]
            hT, hTb = hT_t[i % 2]
            fst, fstb = Fst[i % 2]
            for f in range(24):
                p, pb = pA[nctr[0] % 4], pAb[nctr[0] % 4]
                nctr[0] += 1
                for c in range(8):
                    k.op(PE, lambda e, c=c, f=f, p=p: e.matmul(p[:], lhsT=wqkv[:, c, f * 128:(f + 1) * 128],
                                                                rhs=hT[:, c, :], start=(c == 0), stop=(c == 7)),
                         reads=[hTb], writes=[pb], inc=(c == 7))
                if f % 2 == 0:
                    k.op(ACT, lambda e, f=f, p=p: e.activation(out=fst[:, f, :], in_=p[:], func=AF.Copy),
                         reads=[pb], writes=[fstb])
                else:
                    k.op(DVE, lambda e, f=f, p=p: e.tensor_copy(out=fst[:, f, :], in_=p[:]),
                         reads=[pb], writes=[fstb])
            for qi, key in enumerate(("NQ", "NK", "NV")):
                for hq in range(2):
                    k.dma(SP, SCR[g][key].rearrange("f d w -> d f w")[:, hq * 4:(hq + 1) * 4, blk * 512:(blk + 1) * 512],
                          fst[:, qi * 8 + hq * 4:qi * 8 + (hq + 1) * 4, :], fstb, reads=[fstb], store=True, chain=True)
        nblk = len(blocks)
        issue_load(0)
        if nblk > 1:
            issue_load(1)
        front(0)
        for i in range(nblk):
            if i + 1 < nblk:
                front(i + 1)
            if i + 2 < nblk:
                issue_load(i + 2)
            back(i)
        k.barrier()
        sc.close()

    def phase_na():
        scale = float(64 ** -0.5)
        sc = Scope(k)
        WMAX = GROUPS[0]["W"]
        RMAX = GROUPS[0]["R"]
        qT = [[sc.sb("nqT%d_%d" % (i, hh), [64, WMAX], BF16, dma=True) for hh in range(2)] for i in range(2)]
        kT = [[sc.sb("nkT%d_%d" % (i, hh), [64, WMAX], BF16, dma=True) for hh in range(2)] for i in range(2)]
        vT = [sc.sb("nvT%d" % i, [128, WMAX], BF16, dma=True) for i in range(2)]
        tb = [sc.sb("ntb%d" % i, [128, 14, 2, 64], F32, dma=True) for i in range(2)]
        mask, maskb = sc.sb("nmask", [128, 64], F32, dma=True)
        Vall, Vallb = sc.sb("Vall", [128, RMAX - 1, 128], BF16)
        sbt = [sc.sb("nsb%d" % i, [128, 4, 2, 64], F32) for i in range(3)]
        ptt = [sc.sb("xpt%d" % i, [128, 4, 2, 64], BF16) for i in range(3)]
        itc = [0]
        rd, rdb = sc.sb("nrd", [64, 2, 64], F32)
        Ost = [sc.sb("nOst%d" % i, [64, 2, WMAX], BF16, dma=True) for i in range(2)]
        pV = [sc.ps("npV%d" % i, [128, 1024], BF16) for i in range(2)]
        pVb = [Buf("npV%d" % i) for i in range(2)]
        pS = [sc.ps("npS%d" % i, [128, 512], F32) for i in range(3)]
        pSb = [Buf("npS%d" % i) for i in range(3)]
        pO = [sc.ps("npO%d" % i, [128, 512], F32) for i in range(3)]
        pOb = [Buf("npO%d" % i) for i in range(3)]
        tm = k.dma(SP, mask[:], I["namask"][:, :], maskb, writes=[maskb])
        DVE.wait(tm)

        pairs = []
        for g, cfg in enumerate(GROUPS):
            for f in range(8):
                pairs.append((g, f))

        def load_pair(pi):
            g, f = pairs[pi]
            W = GROUPS[g]["W"]
            for (tt, key) in ((qT, "NQ"), (kT, "NK")):
                for hh in range(2):
                    t, b = tt[pi % 2][hh]
                    k.dma(SP, t[:, 0:W], SCR[g][key][f, hh * 64:(hh + 1) * 64, :], b, writes=[b])
            t, b = vT[pi % 2]
            k.dma(SP, t[:, 0:W], SCR[g]["NV"][f, :, :], b, writes=[b])
            t, b = tb[pi % 2]
            k.dma(SP, t[:], I["nab"][:, f], b, writes=[b])

        load_pair(0)
        it = 0
        for pi, (g, f) in enumerate(pairs):
            if pi + 1 < len(pairs):
                load_pair(pi + 1)
            R, W = GROUPS[g]["R"], GROUPS[g]["W"]
            qh = qT[pi % 2]
            kh = kT[pi % 2]
            v, vb_ = vT[pi % 2]
            tab, tabb = tb[pi % 2]
            for hh in range(2):
                for rr in range(14):
                    k.op(DVE, lambda e, hh=hh, rr=rr: e.tensor_tensor(
                        out=tab[:, rr, hh, :], in0=tab[:, rr, hh, :], in1=mask[:], op=ALU.add),
                        reads=[tabb, maskb], writes=[tabb])
            a = 0
            nv = 0
            while a < R - 1:
                cnt = min(8, R - 1 - a)
                p, pb = pV[nv % 2], pVb[nv % 2]
                nv += 1
                for u in range(cnt):
                    k.op(PE, lambda e, u=u, a=a, p=p: e.transpose(
                        out=p[:, u * 128:(u + 1) * 128], in_=v[:, (a + u) * 64:(a + u) * 64 + 128], identity=ident[:]),
                        reads=[vb_], writes=[pb], inc=(u == cnt - 1))
                k.op(ACT, lambda e, a=a, cnt=cnt, p=p: e.activation(
                    out=Vall[:, a:a + cnt, :], in_=p[:, 0:cnt * 128].rearrange("p (u v) -> p u v", v=128),
                    func=AF.Copy), reads=[pb], writes=[Vallb])
                a += cnt
            its = list(range(R))
            st_ = {}
            ost, ostb = Ost[pi % 2]

            def emit_s(ii):
                r = its[ii]
                r0 = min(max(r - 4, 0), R - 8)
                rr0 = 7 - (r - r0)
                n_ = itc[0]
                itc[0] += 1
                ps, psb = pS[n_ % 3], pSb[n_ % 3]
                sbx, sbxb = sbt[n_ % 3]
                ptx, ptxb = ptt[n_ % 3]
                st_[ii] = (r0, ptx, ptxb, n_)
                for hh in range(2):
                    hb = hh * 64
                    for j in range(4):
                        k.op(PE, lambda e, j=j, hh=hh, hb=hb: e.matmul(
                            ps[:, j * 128 + hh * 64:j * 128 + (hh + 1) * 64],
                            lhsT=kh[hh][0][:, (r0 + 2 * j) * 64:(r0 + 2 * j) * 64 + 128],
                            rhs=qh[hh][0][:, r * 64:(r + 1) * 64], start=True, stop=True),
                            reads=[kh[hh][1], qh[hh][1]], writes=[psb], inc=(hh == 1 and j == 3))
                k.op(DVE, lambda e: e.scalar_tensor_tensor(
                    out=sbx[:].rearrange("p j h c -> p j (h c)"), in0=ps[:, 0:512].rearrange("p (j x) -> p j x", j=4),
                    scalar=scale, in1=tab[:, rr0:rr0 + 7:2, :, :].rearrange("p j h c -> p j (h c)"),
                    op0=ALU.mult, op1=ALU.add),
                    reads=[psb, tabb], writes=[sbxb])
                k.op(ACT, lambda e: e.activation(out=ptx[:], in_=sbx[:], func=AF.Exp),
                     reads=[sbxb], writes=[ptxb])

            def emit_o(ii):
                r = its[ii]
                r0, ptx, ptxb, n_ = st_[ii]
                po, pob = pO[n_ % 3], pOb[n_ % 3]
                for hh in range(2):
                    hb = hh * 64
                    for j in range(4):
                        k.op(PE, lambda e, j=j, hh=hh, hb=hb: e.matmul(
                            po[0:64, hh * 64:(hh + 1) * 64], lhsT=Vall[:, r0 + 2 * j, hb:hb + 64],
                            rhs=ptx[:, j, hh, :], start=(j == 0), stop=(j == 3)),
                            reads=[Vallb, ptxb], writes=[pob], inc=False)
                for j in range(4):
                    k.op(PE, lambda e, j=j: e.matmul(
                        po[0:64, 128:256], lhsT=onesb[:, 0:64], rhs=ptx[:, j, :, :].rearrange("p h c -> p (h c)"),
                        start=(j == 0), stop=(j == 3)), reads=[ptxb], writes=[pob], inc=(j == 3))
                k.op(DVE, lambda e: e.reciprocal(out=rd[:].rearrange("p h c -> p (h c)"), in_=po[0:64, 128:256]),
                     reads=[pob], writes=[rdb])
                k.op(DVE, lambda e: e.tensor_tensor(
                    out=ost[:, :, r * 64:(r + 1) * 64], in0=po[0:64, 0:128].rearrange("p (h c) -> p h c", h=2),
                    in1=rd[:], op=ALU.mult), reads=[pob, rdb], writes=[ostb])
                if r == R - 1:
                    k.dma(SP, SCR[g]["NO"].rearrange("h d w -> d h w")[:, 2 * f:2 * f + 2, :], ost[:, :, 0:W], ostb,
                          reads=[ostb], store=True)

            ni = len(its)
            emit_s(0)
            emit_s(1)
            for ii in range(ni):
                emit_o(ii)
                if ii + 2 < ni:
                    emit_s(ii + 2)
        k.barrier()
        sc.close()

    NP = NPHASES[0]
    if NP >= 1:
        phase_kv()
    else:
        k.barrier()
    if NP >= 2:
        phase_q()
    if NP >= 3:
        phase_att()
    krsc.close()
    if NP >= 4:
        phase_wo("Od", I["w_o"], 16, 128, lambda g: I["xw%d" % g], "X1")
    if NP >= 5:
        phase_mlp(0, "X1", "X2", False)
    if NP >= 6:
        phase_naprep()
    if NP >= 7:
        phase_na()
    if NP >= 9:
        wsc = Scope(k)
        pre = dict(w1=alloc_weight(wsc, "w1p", [128, 8, DFF]), w2=alloc_weight(wsc, "w2p", [128, 32, D]))

        def _prefetch():
            pre["tok1"] = issue_weight(pre["w1"][0], pre["w1"][1],
                                       I["w1"][1].rearrange("(c p) n -> p c n", p=128), 8)
            pre["tok2"] = issue_weight(pre["w2"][0], pre["w2"][1],
                                       I["w2"][1].rearrange("(c p) n -> p c n", p=128), 32)
        phase_wo("NO", I["na_o"], 8, 128, lambda g: SCR[g]["X2"], "X3", prefetch=_prefetch)
        phase_mlp(1, "X3", None, True, pre=pre)
        wsc.close()
    elif NP >= 8:
        phase_wo("NO", I["na_o"], 8, 128, lambda g: SCR[g]["X2"], "X3")
    glob.close()
    es.close()
    return nc


def _rope_tables(S):
    inv = (10000.0 ** (-np.arange(0, 64, 2, dtype=np.float32) / np.float32(64))).astype(np.float32)
    ang = (np.arange(S, dtype=np.float32)[:, None] * inv[None, :]).astype(np.float32)
    return np.cos(ang).astype(np.float32), np.sin(ang).astype(np.float32)


def _win_start(g, idx):
    cfg = GROUPS[g]
    rows = cfg["S"] // GRID_W
    own0 = idx * cfg["OWN"]
    ws = min(max(own0 - 4, 0), rows - cfg["R"])
    return ws, own0 - ws


_NC_CACHE = {}


def _prepare(x_prompt, x_sample, attn_norm, mlp_norm, final_norm, mla_w_dq, mla_q_norm, mla_w_uq,
             mla_w_dkv, mla_kv_norm, mla_w_ukv, mla_w_o, na_w_qkv, na_rpb, na_w_o, mlp_w1, mlp_w2):
    f32 = np.float32
    x_prompt = np.asarray(x_prompt, f32)
    x_sample = np.asarray(x_sample, f32)
    xs = [x_prompt, x_sample]
    ncores = 8
    gains = np.zeros((GW,), f32)
    gains[G_ATTN0:G_ATTN0 + D] = np.asarray(attn_norm)[0]
    gains[G_MLP0:G_MLP0 + D] = np.asarray(mlp_norm)[0]
    gains[G_ATTN1:G_ATTN1 + D] = np.asarray(attn_norm)[1]
    gains[G_MLP1:G_MLP1 + D] = np.asarray(mlp_norm)[1]
    gains[G_FIN:G_FIN + D] = np.asarray(final_norm)
    gains[G_QN:G_QN + 384] = np.asarray(mla_q_norm)[0]
    gains[G_KVN:G_KVN + 256] = np.asarray(mla_kv_norm)[0]
    gains = np.ascontiguousarray(np.broadcast_to(gains[None, :], (128, GW)))
    wuq = np.asarray(mla_w_uq, f32)[0].reshape(384, NH, 192)
    sw = np.concatenate([wuq[:, :, 160:192], wuq[:, :, 128:160]], axis=2)
    wuq2 = np.concatenate([wuq[:, :, 0:192], wuq[:, :, 128:192], sw, sw], axis=2)
    wuq2 = np.ascontiguousarray(wuq2.reshape(384, NH * 384))
    wukv = np.asarray(mla_w_ukv, f32)[0].reshape(256, NH, 256)
    w_uk = np.ascontiguousarray(wukv[:, :, 0:128].reshape(256, NH * 128))
    w_uv = np.ascontiguousarray(wukv[:, :, 128:256].reshape(256, NH * 128))
    rpb = np.asarray(na_rpb, f32)[0]
    p = np.arange(128)
    kc = p % 64
    hi = p // 64
    c = np.arange(64)
    rel = kc[:, None] - c[None, :] + 15
    cs = np.clip(c - 8, 0, 48)
    inwin = (kc[:, None] >= cs[None, :]) & (kc[:, None] < cs[None, :] + 16)
    relc = np.clip(rel, 0, 30)
    rr = np.arange(14)
    nab = rpb[:, (rr[None, :, None] + hi[:, None, None]), relc[:, None, :]]
    nab = np.transpose(nab, (1, 0, 2, 3)).reshape(128, 8, 2, 14, 64)
    nab = np.ascontiguousarray(np.transpose(nab, (0, 1, 3, 2, 4))).astype(f32)
    namask = np.where(inwin, 0.0, NEG).astype(f32)
    shared = dict(
        gains=gains, ident=np.eye(128, dtype=f32),
        w_dkv=np.ascontiguousarray(np.asarray(mla_w_dkv, f32)[0]),
        w_dq=np.ascontiguousarray(np.asarray(mla_w_dq, f32)[0]),
        w_uq=wuq2, w_uk=w_uk, w_uv=w_uv,
        w_o=np.ascontiguousarray(np.asarray(mla_w_o, f32)[0]),
        w1=np.ascontiguousarray(np.asarray(mlp_w1, f32)), w2=np.ascontiguousarray(np.asarray(mlp_w2, f32)),
        na_qkv=np.ascontiguousarray(np.asarray(na_w_qkv, f32)[0]),
        na_o=np.ascontiguousarray(np.asarray(na_w_o, f32)[0]),
        nab=nab, namask=namask,
    )
    ropek = []
    rope_full = []
    for g, cfg in enumerate(GROUPS):
        S, NT = cfg["S"], cfg["NT"]
        cos, sin = _rope_tables(S)
        rope_full.append((cos, sin))
        t = np.concatenate([cos, cos, -sin, sin], axis=1)
        ropek.append(np.ascontiguousarray(t.reshape(128, NT, 128)))
    in_maps = []
    meta = []
    for cidx in range(ncores):
        m = dict(shared)
        mm = []
        for g, cfg in enumerate(GROUPS):
            per = ncores // xs[g].shape[0]
            b, idx = cidx // per, cidx % per
            ws, off = _win_start(g, idx)
            t0 = ws * GRID_W
            W = cfg["W"]
            m["xs%d" % g] = np.ascontiguousarray(xs[g][b])
            m["xw%d" % g] = np.ascontiguousarray(xs[g][b, t0:t0 + W])
            m["ropek%d" % g] = ropek[g]
            cos, sin = rope_full[g]
            cw, sw = cos[t0:t0 + W].T, sin[t0:t0 + W].T
            rq = np.stack([np.concatenate([cw, cw, cw, cw], 0), np.concatenate([-sw, sw, -sw, sw], 0)], axis=1)
            m["ropeq%d" % g] = np.ascontiguousarray(rq.astype(f32))
            mm.append((b, idx, off))
        in_maps.append(m)
        meta.append(mm)
    return in_maps, meta, (x_prompt.shape, x_sample.shape)


def kernel(**inputs):
    f32 = np.float32
    in_maps, meta, shapes = _prepare(**inputs)
    ncores = len(in_maps)
    if "nc" not in _NC_CACHE:
        _NC_CACHE["nc"] = build_program()
    nc = _NC_CACHE["nc"]
    res = run_bass_kernel_spmd(nc, in_maps, core_ids=list(range(ncores)))
    outs = [np.zeros(shapes[0], f32), np.zeros(shapes[1], f32)]
    for cidx in range(ncores):
        r = res.results[cidx]
        for g, cfg in enumerate(GROUPS):
            b, idx, off = meta[cidx][g]
            own = cfg["OWN"] * GRID_W
            y = np.asarray(r["y%d" % g])
            outs[g][b, idx * own:(idx + 1) * own] = y[off * GRID_W:off * GRID_W + own]
    return (outs[0], outs[1])
```
